# Optimizing a Trainium2 kernel written in Bass

```python
import math
import jax, jax.numpy as jnp
from jax import lax
import numpy as np

D_MODEL = 1024
BATCH = 16
SEQ = 2048
DEPTH = 4
DEC_BATCH = 2
DEC_SEQ = 16384
PAST_LEN = 128

RG_WIDTH = 512
RG_BLOCKS = 8
RG_BLOCK = RG_WIDTH // RG_BLOCKS
RG_CONV = 4
RG_CONV_LEFT = 2
RG_C = 8.0
N_HEADS = 8
N_KV_HEADS = 2
HEAD_DIM = 64
Q_GROUP = N_HEADS // N_KV_HEADS
ATTN_WIDTH = N_HEADS * HEAD_DIM
KV_WIDTH = N_KV_HEADS * HEAD_DIM
WINDOW = 128
BLOCK = 128
N_BUCKETS = 32
MAX_DISTANCE = 128
D_FF = 2816
FFN_CONV = 3
FFN_CONV_LEFT = 1
MIX_WIDTH = RG_WIDTH + ATTN_WIDTH
IN_COLS = 2 * RG_WIDTH + ATTN_WIDTH + 2 * KV_WIDTH
SPLITS = (RG_WIDTH, 2 * RG_WIDTH, 2 * RG_WIDTH + ATTN_WIDTH, 2 * RG_WIDTH + ATTN_WIDTH + KV_WIDTH)
ALPHA = (2 * DEPTH) ** 0.25
BETA = (8 * DEPTH) ** -0.25
LN_EPS = 1e-5
RMS_EPS = 1e-6
NEG = -1e30

kernel_name = 'hymba_rglru_swa_deepnorm_encoder'


def layer_norm(x, g, b):
    xf = x.astype(jnp.float32)
    mu = jnp.mean(xf, axis=-1, keepdims=True)
    var = jnp.mean(jnp.square(xf - mu), axis=-1, keepdims=True)
    y = (xf - mu) * lax.rsqrt(var + LN_EPS) * g.astype(jnp.float32) + b.astype(jnp.float32)
    return y.astype(x.dtype)


def rms_norm(x, g):
    xf = x.astype(jnp.float32)
    y = xf * lax.rsqrt(jnp.mean(xf * xf, axis=-1, keepdims=True) + RMS_EPS) * g.astype(jnp.float32)
    return y.astype(x.dtype)


def depthwise_conv(x, w, b, left):
    k_width = w.shape[0]
    s = x.shape[1]
    xp = jnp.pad(x, ((0, 0), (left, k_width - 1 - left), (0, 0)))
    y = b
    for k in range(k_width):
        y = y + xp[:, k:k + s] * w[k]
    return y


def _combine(left, right):
    a_l, b_l = left
    a_r, b_r = right
    return a_l * a_r, a_r * b_l + b_r


def linear_scan(a, b, reverse):
    _, h = lax.associative_scan(_combine, (a, b), reverse=reverse, axis=1)
    return h


def rglru_bidir(xr, wa, ba, wx, bx, lam):
    bsz, s, _ = xr.shape
    xb = xr.reshape(bsz, s, RG_BLOCKS, RG_BLOCK)
    ga = jnp.einsum('bsnc,enco->ebsno', xb, wa).reshape(2, bsz, s, RG_WIDTH) + ba[:, None, None, :]
    gx = jnp.einsum('bsnc,enco->ebsno', xb, wx).reshape(2, bsz, s, RG_WIDTH) + bx[:, None, None, :]
    r = jax.nn.sigmoid(ga.astype(jnp.float32))
    i = jax.nn.sigmoid(gx.astype(jnp.float32))
    log_a = -RG_C * r * jax.nn.softplus(-lam.astype(jnp.float32))[:, None, None, :]
    a = jnp.exp(log_a)
    bterm = jnp.sqrt(-jnp.expm1(2.0 * log_a)) * (i * xr.astype(jnp.float32)[None])
    h_fwd = linear_scan(a[0], bterm[0], reverse=False)
    h_bwd = linear_scan(a[1], bterm[1], reverse=True)
    return h_fwd + h_bwd


def _band_structure():
    i = np.arange(BLOCK)[:, None]
    c = np.arange(3 * BLOCK)[None, :]
    rel = (c - BLOCK) - i
    half = N_BUCKETS // 2
    exact = half // 2
    n = np.abs(rel)
    large = exact + (np.log(np.maximum(n, 1) / exact) / np.log(MAX_DISTANCE / exact) * (half - exact)).astype(np.int32)
    large = np.minimum(large, half - 1)
    bucket = np.where(n < exact, n, large) + (rel > 0).astype(np.int32) * half
    return bucket.astype(np.int32), n <= WINDOW


def windowed_attention(q, k, v, sink, band_bias, band_mask):
    bsz, s = q.shape[0], q.shape[1]
    nb = s // BLOCK
    qb = q.reshape(bsz, nb, BLOCK, N_KV_HEADS, Q_GROUP, HEAD_DIM)

    def windows(t):
        tp = jnp.pad(t, ((0, 0), (BLOCK, BLOCK), (0, 0), (0, 0)))
        tp = tp.reshape(bsz, nb + 2, BLOCK, N_KV_HEADS, HEAD_DIM)
        return jnp.concatenate([tp[:, :-2], tp[:, 1:-1], tp[:, 2:]], axis=2)

    kw = windows(k)
    vw = windows(v)
    scale = 1.0 / math.sqrt(HEAD_DIM)
    sc = jnp.einsum('bnqhgd,bnkhd->bnhgqk', qb, kw).astype(jnp.float32) * scale
    sc = sc + band_bias.astype(jnp.float32).reshape(N_KV_HEADS, Q_GROUP, BLOCK, 3 * BLOCK)
    key_pos = np.arange(nb)[:, None] * BLOCK + np.arange(3 * BLOCK)[None, :] - BLOCK
    valid = ((key_pos >= 0) & (key_pos < s))[:, None, :] & band_mask[None]
    sc = jnp.where(valid[None, :, None, None], sc, NEG)
    sink_f = sink.astype(jnp.float32).reshape(N_KV_HEADS, Q_GROUP)[:, :, None, None]
    m = jnp.maximum(jnp.max(sc, axis=-1, keepdims=True), sink_f)
    p = jnp.exp(sc - m)
    denom = jnp.sum(p, axis=-1, keepdims=True) + jnp.exp(sink_f - m)
    p = (p / denom).astype(vw.dtype)
    o = jnp.einsum('bnhgqk,bnkhd->bnqhgd', p, vw)
    return o.reshape(bsz, s, ATTN_WIDTH).astype(q.dtype)


def hybrid_mixer(x, w_in, rg_conv_w, rg_conv_b, rg_wa, rg_ba, rg_wx, rg_bx, rg_lambda,
                 attn_sink, band_bias, band_mask, norm_rg_g, norm_attn_g, w_out):
    bsz, s, _ = x.shape
    proj = x @ w_in
    xr, yr, q, k, v = jnp.split(proj, SPLITS, axis=-1)
    xr = depthwise_conv(xr, rg_conv_w, rg_conv_b, RG_CONV_LEFT)
    h = rglru_bidir(xr, rg_wa, rg_ba, rg_wx, rg_bx, rg_lambda)
    rg_out = (h * jax.nn.gelu(yr.astype(jnp.float32))).astype(x.dtype)
    attn_out = windowed_attention(q.reshape(bsz, s, N_HEADS, HEAD_DIM),
                                  k.reshape(bsz, s, N_KV_HEADS, HEAD_DIM),
                                  v.reshape(bsz, s, N_KV_HEADS, HEAD_DIM),
                                  attn_sink, band_bias, band_mask)
    mix = jnp.concatenate([rms_norm(rg_out, norm_rg_g), rms_norm(attn_out, norm_attn_g)], axis=-1)
    return mix @ w_out


def conv_ffn(x, ffn_w_in, ffn_conv_w, ffn_conv_b, ffn_w_out):
    g, u = jnp.split(x @ ffn_w_in, 2, axis=-1)
    g = depthwise_conv(g, ffn_conv_w, ffn_conv_b, FFN_CONV_LEFT)
    return (jax.nn.gelu(g) * u) @ ffn_w_out


def run_trunk(x, w_in, rg_conv_w, rg_conv_b, rg_wa, rg_ba, rg_wx, rg_bx, rg_lambda,
              attn_sink, rel_bias, norm_rg_g, norm_attn_g, w_out, ln1_g, ln1_b,
              ffn_w_in, ffn_conv_w, ffn_conv_b, ffn_w_out, ln2_g, ln2_b):
    buckets, band_mask = _band_structure()
    band_bias = jnp.transpose(rel_bias[buckets], (2, 0, 1))
    for l in range(DEPTH):
        mix = hybrid_mixer(x, w_in[l], rg_conv_w[l], rg_conv_b[l], rg_wa[l], rg_ba[l], rg_wx[l], rg_bx[l],
                           rg_lambda[l], attn_sink[l], band_bias, band_mask, norm_rg_g[l], norm_attn_g[l], w_out[l])
        x = layer_norm(ALPHA * x + mix, ln1_g[l], ln1_b[l])
        f = conv_ffn(x, ffn_w_in[l], ffn_conv_w[l], ffn_conv_b[l], ffn_w_out[l])
        x = layer_norm(ALPHA * x + f, ln2_g[l], ln2_b[l])
    return x


def setup_inputs(seed: int = 0) -> dict:
    key = jax.random.key(seed)
    ks = jax.random.split(key, 24)
    f32 = jnp.float32
    nrm = lambda k, shape, s: jax.random.normal(k, shape, f32) * s
    x_prompt = jax.random.normal(ks[0], (BATCH, SEQ, D_MODEL), f32)
    x_sample = jax.random.normal(ks[1], (DEC_BATCH, DEC_SEQ, D_MODEL), f32)
    w_in = nrm(ks[2], (DEPTH, D_MODEL, IN_COLS), D_MODEL ** -0.5)
    w_in = w_in.at[..., SPLITS[3]:].multiply(BETA)
    rg_conv_w = nrm(ks[3], (DEPTH, RG_CONV, RG_WIDTH), RG_CONV ** -0.5)
    rg_conv_b = nrm(ks[4], (DEPTH, RG_WIDTH), 0.01)
    rg_wa = nrm(ks[5], (DEPTH, 2, RG_BLOCKS, RG_BLOCK, RG_BLOCK), RG_BLOCK ** -0.5)
    rg_ba = nrm(ks[6], (DEPTH, 2, RG_WIDTH), 0.01)
    rg_wx = nrm(ks[7], (DEPTH, 2, RG_BLOCKS, RG_BLOCK, RG_BLOCK), RG_BLOCK ** -0.5)
    rg_bx = nrm(ks[8], (DEPTH, 2, RG_WIDTH), 0.01)
    a_c = jax.random.uniform(ks[9], (DEPTH, 2, RG_WIDTH), f32, minval=0.9, maxval=0.999)
    a0 = a_c ** (1.0 / RG_C)
    rg_lambda = jnp.log(a0) - jnp.log1p(-a0)
    attn_sink = nrm(ks[10], (DEPTH, N_HEADS), 0.5)
    rel_bias = nrm(ks[11], (N_BUCKETS, N_HEADS), 0.2)
    norm_rg_g = 1.0 + nrm(ks[12], (DEPTH, RG_WIDTH), 0.02)
    norm_attn_g = 1.0 + nrm(ks[13], (DEPTH, ATTN_WIDTH), 0.02)
    w_out = nrm(ks[14], (DEPTH, MIX_WIDTH, D_MODEL), BETA * MIX_WIDTH ** -0.5)
    ln1_g = 1.0 + nrm(ks[15], (DEPTH, D_MODEL), 0.02)
    ln1_b = nrm(ks[16], (DEPTH, D_MODEL), 0.02)
    ffn_w_in = nrm(ks[17], (DEPTH, D_MODEL, 2 * D_FF), BETA * D_MODEL ** -0.5)
    ffn_conv_w = nrm(ks[18], (DEPTH, FFN_CONV, D_FF), FFN_CONV ** -0.5)
    ffn_conv_b = nrm(ks[19], (DEPTH, D_FF), 0.01)
    ffn_w_out = nrm(ks[20], (DEPTH, D_FF, D_MODEL), BETA * D_FF ** -0.5)
    ln2_g = 1.0 + nrm(ks[21], (DEPTH, D_MODEL), 0.02)
    ln2_b = nrm(ks[22], (DEPTH, D_MODEL), 0.02)
    return {'x_prompt': x_prompt, 'x_sample': x_sample, 'w_in': w_in, 'rg_conv_w': rg_conv_w,
            'rg_conv_b': rg_conv_b, 'rg_wa': rg_wa, 'rg_ba': rg_ba, 'rg_wx': rg_wx, 'rg_bx': rg_bx,
            'rg_lambda': rg_lambda, 'attn_sink': attn_sink, 'rel_bias': rel_bias, 'norm_rg_g': norm_rg_g,
            'norm_attn_g': norm_attn_g, 'w_out': w_out, 'ln1_g': ln1_g, 'ln1_b': ln1_b,
            'ffn_w_in': ffn_w_in, 'ffn_conv_w': ffn_conv_w, 'ffn_conv_b': ffn_conv_b,
            'ffn_w_out': ffn_w_out, 'ln2_g': ln2_g, 'ln2_b': ln2_b}


def reference(x_prompt, x_sample, w_in, rg_conv_w, rg_conv_b, rg_wa, rg_ba, rg_wx, rg_bx, rg_lambda,
              attn_sink, rel_bias, norm_rg_g, norm_attn_g, w_out, ln1_g, ln1_b,
              ffn_w_in, ffn_conv_w, ffn_conv_b, ffn_w_out, ln2_g, ln2_b):
    y_prompt = run_trunk(x_prompt, w_in, rg_conv_w, rg_conv_b, rg_wa, rg_ba, rg_wx, rg_bx, rg_lambda,
                         attn_sink, rel_bias, norm_rg_g, norm_attn_g, w_out, ln1_g, ln1_b,
                         ffn_w_in, ffn_conv_w, ffn_conv_b, ffn_w_out, ln2_g, ln2_b)
    y_sample = run_trunk(x_sample, w_in, rg_conv_w, rg_conv_b, rg_wa, rg_ba, rg_wx, rg_bx, rg_lambda,
                         attn_sink, rel_bias, norm_rg_g, norm_attn_g, w_out, ln1_g, ln1_b,
                         ffn_w_in, ffn_conv_w, ffn_conv_b, ffn_w_out, ln2_g, ln2_b)
    return (y_prompt, y_sample)
```

```python
import numpy as np
import concourse.bass as bass
import concourse.mybir as mybir
from concourse.bass_utils import run_bass_kernel_spmd

F32 = mybir.dt.float32
BF16 = mybir.dt.bfloat16
AF = mybir.ActivationFunctionType
ALU = mybir.AluOpType
AX = mybir.AxisListType

COMPUTE = ("pe", "act", "dve", "pool")


class _Op:
    __slots__ = ("eng", "fn", "deps", "signal", "val", "dma_sem", "kind", "name")

    def __init__(self, eng, fn, kind, name=""):
        self.eng = eng
        self.fn = fn
        self.deps = []
        self.signal = False
        self.val = None
        self.dma_sem = None
        self.kind = kind
        self.name = name


class Prog:
    def __init__(self, nc):
        self.nc = nc
        self.queues = {e: [] for e in ("pe", "act", "dve", "pool", "sp")}
        self.last_w = {}
        self.readers = {}
        self.dma_sem_names = []
        self.all_ops = []
        self._pending = {}

    def _deps(self, op, reads, writes):
        deps = []
        for k in reads:
            w = self.last_w.get(k)
            if w is not None:
                deps.append(w)
        for k in writes:
            w = self.last_w.get(k)
            if w is not None:
                deps.append(w)
            for r in self.readers.get(k, ()):
                deps.append(r)
        seen = set()
        for d in deps:
            if d is op or id(d) in seen:
                continue
            seen.add(id(d))
            if d.kind == "c" and op.kind == "c" and d.eng == op.eng and d.eng == "pe":
                continue
            op.deps.append(d)
            d.signal = True
        pend = self._pending.pop(op.eng, None)
        if pend:
            for d in pend:
                if d is op or id(d) in seen:
                    continue
                if d.kind == "c" and d.eng == op.eng and op.kind == "c" and d.eng == "pe":
                    continue
                seen.add(id(d))
                op.deps.append(d)
                d.signal = True
        for k in reads:
            self.readers.setdefault(k, []).append(op)
        for k in writes:
            self.last_w[k] = op
            self.readers[k] = []

    def op(self, eng, fn, reads=(), writes=(), name=""):
        o = _Op(eng, fn, "c", name)
        self._deps(o, reads, writes)
        self.queues[eng].append(o)
        self.all_ops.append(o)
        return o

    def dma(self, queue, sem, out, in_, reads=(), writes=(), **kw):
        if sem not in self.dma_sem_names:
            self.dma_sem_names.append(sem)
        o = _Op(queue, None, "d", sem)
        o.dma_sem = sem
        o.fn = lambda e, out=out, in_=in_, kw=kw: e.dma_start(out=out, in_=in_, **kw)
        self._deps(o, reads, writes)
        self.queues[queue].append(o)
        self.all_ops.append(o)
        return o

    def barrier(self):
        tails = []
        for q in self.queues.values():
            last_c = None
            for o in q:
                if o.kind == "c":
                    last_c = o
            if last_c is not None:
                tails.append(last_c)
        last_dma = {}
        for o in self.all_ops:
            if o.kind == "d":
                last_dma[o.dma_sem] = o
        tails.extend(last_dma.values())
        self.last_w = {}
        self.readers = {}
        for t in tails:
            t.signal = True
        self._pending = {e: list(tails) for e in self.queues}

    def emit(self):
        nc = self.nc
        engs = {"pe": nc.tensor, "act": nc.scalar, "dve": nc.vector, "pool": nc.gpsimd, "sp": nc.sync}
        cnt = {e: 0 for e in COMPUTE}
        dcnt = {s: 0 for s in self.dma_sem_names}
        for o in self.all_ops:
            if o.kind == "d":
                dcnt[o.dma_sem] += 16
                o.val = dcnt[o.dma_sem]
            elif o.signal:
                cnt[o.eng] += 1
                o.val = cnt[o.eng]
        import contextlib
        with contextlib.ExitStack() as st:
            csem = {e: st.enter_context(nc.semaphore("s_" + e)) for e in COMPUTE}
            dsem = {s: st.enter_context(nc.semaphore("d_" + s)) for s in self.dma_sem_names}
            block = st.enter_context(nc.Block())

            def semof(o):
                return dsem[o.dma_sem] if o.kind == "d" else csem[o.eng]

            def run_queue(ename, e):
                known = {}
                for o in self.queues[ename]:
                    need = {}
                    for d in o.deps:
                        s = ("d", d.dma_sem) if d.kind == "d" else ("c", d.eng)
                        if need.get(s, 0) < d.val:
                            need[s] = d.val
                    for s, v in need.items():
                        if known.get(s, 0) >= v:
                            continue
                        known[s] = v
                        e.wait_ge(dsem[s[1]] if s[0] == "d" else csem[s[1]], v)
                    ins = o.fn(e)
                    if o.kind == "d":
                        ins.then_inc(dsem[o.dma_sem], 16)
                    elif o.signal:
                        ins.then_inc(csem[o.eng], 1)
                if ename in final_waits:
                    for o in final_waits[ename]:
                        e.wait_ge(semof(o), o.val)

            last_dma = {}
            for o in self.all_ops:
                if o.kind == "d":
                    last_dma[o.dma_sem] = o
            final_waits = {"sp": list(last_dma.values())}

            @block.tensor
            def _(e):
                run_queue("pe", e)

            @block.scalar
            def _(e):
                run_queue("act", e)

            @block.vector
            def _(e):
                run_queue("dve", e)

            @block.gpsimd
            def _(e):
                run_queue("pool", e)

            @block.sync
            def _(e):
                run_queue("sp", e)


D = 1024
RGW = 512
INC = 1792
DFF = 2816
NFF = 22
ALPHA = float(8 ** 0.25)
LN_EPS = 1e-5
RMS_EPS = 1e-6
NEGB = -30000.0
_CUT = 99


def _band_onehot():
    half, exact = 16, 8
    r = np.arange(3)[:, None, None]
    q = np.arange(128)[None, :, None]
    k = np.arange(128)[None, None, :]
    rel = (r - 1) * 128 + k - q
    n = np.abs(rel)
    large = exact + (np.log(np.maximum(n, 1) / exact) / np.log(128 / exact) * (half - exact)).astype(np.int32)
    large = np.minimum(large, half - 1)
    bucket = np.where(n < exact, n, large) + (rel > 0).astype(np.int32) * half
    bucket = np.where(n <= 128, bucket, 32)
    oh = (bucket[None] == np.arange(33)[:, None, None, None]).astype(np.float32)
    return np.ascontiguousarray(oh.reshape(33, 3 * 128 * 128))


class Arena:
    def __init__(self, ap, nwords):
        self.ap = ap
        self.n = nwords
        self.top = 0

    def f32(self, *shape):
        n = int(np.prod(shape))
        off = self.top
        self.top += n
        assert self.top <= self.n, ("arena overflow", self.top)
        v = self.ap[:, off:off + n]
        return self._shape(v, shape)

    def bf16(self, *shape):
        n = int(np.prod(shape))
        w = (n + 1) // 2
        off = self.top
        self.top += w
        assert self.top <= self.n, ("arena overflow", self.top)
        v = self.ap[:, off:off + w].bitcast(BF16)[:, 0:n]
        return self._shape(v, shape)

    @staticmethod
    def _shape(v, shape):
        if len(shape) == 1:
            return v
        if len(shape) == 2:
            return v.rearrange("p (a b) -> p a b", a=shape[0])
        if len(shape) == 3:
            return v.rearrange("p (a b c) -> p a b c", a=shape[0], b=shape[1])
        raise ValueError(shape)


def build_program(seqs, depth, debug=False):
    NT = sum(n for _, n in seqs)
    nc = bass.Bass("TRN2", target_bir_lowering=False)

    def din(name, shape, dt=F32):
        return nc.dram_tensor(name, list(shape), dt, kind="ExternalInput").ap()

    skind = "ExternalOutput" if debug else "Internal"

    def dscr(name, shape, dt):
        return nc.dram_tensor(name, list(shape), dt, kind=skind).ap()

    x_d = din("x", [NT, D])
    w_in_d = din("w_in", [depth, D, INC])
    rg_conv_w_d = din("rg_conv_w", [depth, 4, RGW])
    rg_conv_b_d = din("rg_conv_b", [depth, RGW])
    rg_wa_d = din("rg_wa", [depth, 2, 8, 64, 64])
    rg_ba_d = din("rg_ba", [depth, 2, RGW])
    rg_wx_d = din("rg_wx", [depth, 2, 8, 64, 64])
    rg_bx_d = din("rg_bx", [depth, 2, RGW])
    rg_lambda_d = din("rg_lambda", [depth, 2, RGW])
    attn_sink_d = din("attn_sink", [depth, 8])
    rel_bias_d = din("rel_bias", [32, 8])
    norm_rg_g_d = din("norm_rg_g", [depth, RGW])
    norm_attn_g_d = din("norm_attn_g", [depth, RGW])
    w_out_d = din("w_out", [depth, D, D])
    ln1_g_d = din("ln1_g", [depth, D])
    ln1_b_d = din("ln1_b", [depth, D])
    ffn_w_in_d = din("ffn_w_in", [depth, D, 2 * DFF])
    ffn_conv_w_d = din("ffn_conv_w", [depth, 3, DFF])
    ffn_conv_b_d = din("ffn_conv_b", [depth, DFF])
    ffn_w_out_d = din("ffn_w_out", [depth, DFF, D])
    ln2_g_d = din("ln2_g", [depth, D])
    ln2_b_d = din("ln2_b", [depth, D])
    ident_d = din("c_ident", [128, 128])
    oh_d = din("c_onehot", [33, 3 * 128 * 128])

    y_d = nc.dram_tensor("y", [NT, D], F32, kind="ExternalOutput").ap()

    xT_d = dscr("s_xT", [D, NT], BF16)
    xres_d = dscr("s_xres", [NT, D], F32)
    xrT_d = dscr("s_xrT", [RGW, NT], F32)
    gyT_d = dscr("s_gyT", [RGW, NT], F32)
    qT_d = dscr("s_qT", [RGW, NT], BF16)
    kT_d = dscr("s_kT", [128, NT], BF16)
    v_d = dscr("s_v", [NT, 128], BF16)
    hbT_d = dscr("s_hbT", [RGW, NT], F32)
    x1_d = dscr("s_x1", [NT, D], F32)
    x1T_d = dscr("s_x1T", [D, NT], BF16)

    P = Prog(nc)
    import contextlib
    with contextlib.ExitStack() as st:
        ar_t = st.enter_context(nc.sbuf_tensor("arena", [128, 49152], F32))
        ps = st.enter_context(nc.psum_tensor("ps", [128, 8, 512], F32))
        A = Arena(ar_t[:, :], 49152)

        def psbf(bank):
            return ps[:, bank, :].bitcast(BF16).rearrange("p (c n) -> p c n", c=8)

        ident = A.bf16(128)
        ones_bf = A.bf16(2)
        biasT = A.bf16(3, 2, 512)
        esink = A.f32(depth, 8)
        rb_bf = A.bf16(8)
        eps_ln = A.f32(1)
        eps_rms = A.f32(1)
        base_mark = A.top
        P.op("pool", lambda e: e.memset(eps_ln, LN_EPS), writes=["eps"])
        P.op("pool", lambda e: e.memset(eps_rms, RMS_EPS), writes=["eps"])

        tmp_id = A.f32(128)
        P.dma("sp", "c0", tmp_id, ident_d, writes=["tmp_id"])
        P.op("dve", lambda e: e.tensor_copy(out=ident, in_=tmp_id), reads=["tmp_id"], writes=["ident"])
        P.op("pool", lambda e: e.memset(ones_bf, 1.0), writes=["ones"])
        P.dma("sp", "c1", esink.rearrange("p l h -> p (l h)"),
              attn_sink_d.rearrange("l h -> (l h)").partition_broadcast(128), writes=["esink"])
        P.op("act", lambda e: e.activation(out=esink, in_=esink, func=AF.Exp), reads=["esink"], writes=["esink"])
        rb32 = A.f32(8)
        P.op("pool", lambda e: e.memset(rb32[0:64, :], NEGB), writes=["rb32"])
        P.dma("sp", "c2", rb32[0:32, :], rel_bias_d, reads=["rb32"], writes=["rb32"])
        P.op("dve", lambda e: e.tensor_copy(out=rb_bf[0:64, :], in_=rb32[0:64, :]), reads=["rb32"], writes=["rb_bf"])
        oh32 = A.f32(128 * 128)
        ohbf = A.bf16(128 * 128)
        for r in range(3):
            P.dma("sp", "c3", oh32[0:33, :], oh_d[:, r * 16384:(r + 1) * 16384], reads=["oh32"], writes=["oh32"])
            P.op("act", lambda e: e.activation(out=ohbf[0:33, :], in_=oh32[0:33, :], func=AF.Copy),
                 reads=["oh32"], writes=["ohbf"])
            for hb in range(2):
                def mm(e, hb=hb):
                    ins = None
                    for qq in range(64):
                        q = hb * 64 + qq
                        ins = e.matmul(ps[:, hb, qq * 8:(qq + 1) * 8], lhsT=ohbf[0:33, q * 128:(q + 1) * 128],
                                       rhs=rb_bf[0:33, :], start=True, stop=True)
                    return ins
                P.op("pe", mm, reads=["ohbf", "rb_bf"], writes=["psb%d" % hb])
            for hg in range(2):
                for hb in range(2):
                    src = ps[:, hb, :].rearrange("p (q h) -> p h q", h=8)[:, hg * 4:(hg + 1) * 4, :]
                    dst = biasT[:, r, hg, :].rearrange("p (h q) -> p h q", h=4)[:, :, hb * 64:(hb + 1) * 64]
                    P.op("dve", lambda e, src=src, dst=dst: e.tensor_copy(out=dst, in_=src),
                         reads=["psb%d" % hb], writes=["biasT"])
        P.barrier()
        A.top = base_mark

        tiles512 = []
        for (s0, n) in seqs:
            for t in range(s0, s0 + n, 512):
                tiles512.append((t, s0, s0 + n))

        def transpose_store(src_bf, psbank, dstT, col0, key_src, key_ps, key_dst, evac_eng):
            pv = psbf(psbank)

            def tr(e):
                ins = None
                for c in range(8):
                    ins = e.transpose(out=pv[:, c, :], in_=src_bf[:, c * 128:(c + 1) * 128], identity=ident)
                return ins
            P.op("pe", tr, reads=[key_src, "ident"], writes=[key_ps])
            if evac_eng == "act":
                P.op("act", lambda e: e.activation(out=dstT[:, :, col0:col0 + 128], in_=pv, func=AF.Copy),
                     reads=[key_ps], writes=[key_dst])
            else:
                P.op("dve", lambda e: e.tensor_copy(out=dstT[:, :, col0:col0 + 128], in_=pv),
                     reads=[key_ps], writes=[key_dst])

        def stage0():
            m = A.top
            xin = [A.f32(4, D) for _ in range(2)]
            xbf = [A.bf16(4, D) for _ in range(2)]
            xTs = [A.bf16(8, 512) for _ in range(2)]
            for gi, (t0, _, _) in enumerate(tiles512):
                sl = gi % 2
                P.dma("sp", "xin%d" % sl, xin[sl], x_d[t0:t0 + 512, :].rearrange("(s p) d -> p s d", p=128),
                      writes=["xin%d" % sl])
                for s in range(4):
                    if s % 2 == 0:
                        P.op("act", lambda e, sl=sl, s=s: e.activation(out=xbf[sl][:, s, :], in_=xin[sl][:, s, :], func=AF.Copy),
                             reads=["xin%d" % sl], writes=["xbf%d_%d" % (sl, s)])
                    else:
                        P.op("dve", lambda e, sl=sl, s=s: e.tensor_copy(out=xbf[sl][:, s, :], in_=xin[sl][:, s, :]),
                             reads=["xin%d" % sl], writes=["xbf%d_%d" % (sl, s)])
                    transpose_store(xbf[sl][:, s, :], s % 2, xTs[sl], s * 128, "xbf%d_%d" % (sl, s),
                                    "pst%d" % (s % 2), "xTs%d" % sl, "dve" if s % 2 == 0 else "act")
                P.dma("pool", "xTs%d" % sl, xT_d[:, t0:t0 + 512].rearrange("(c p) n -> p c n", p=128), xTs[sl],
                      reads=["xTs%d" % sl])
            P.barrier()
            A.top = m

        def stageA(l):
            m = A.top
            w_bf = A.bf16(8, INC)
            stg = [A.f32(INC) for _ in range(2)]
            for k in range(8):
                sl = k % 2
                P.dma("sp", "wst%d" % sl, stg[sl], w_in_d[l, k * 128:(k + 1) * 128, :], writes=["wst%d" % sl])
                if k % 2 == 0:
                    P.op("act", lambda e, k=k, sl=sl: e.activation(out=w_bf[:, k, :], in_=stg[sl], func=AF.Copy),
                         reads=["wst%d" % sl], writes=["w_bf"])
                else:
                    P.op("dve", lambda e, k=k, sl=sl: e.tensor_copy(out=w_bf[:, k, :], in_=stg[sl]),
                         reads=["wst%d" % sl], writes=["w_bf"])
            xTt = [A.bf16(8, 512) for _ in range(2)]
            xr_s = [A.f32(4, 512) for _ in range(2)]
            gy_s = [A.f32(4, 512) for _ in range(2)]
            q_s = [A.bf16(4, 512) for _ in range(2)]
            k_s = [A.bf16(512) for _ in range(2)]
            v_s = [A.bf16(4, 128) for _ in range(2)]
            jglob = 0
            for gi, (t0, _, _) in enumerate(tiles512):
                sl = gi % 2
                P.dma("sp", "xTt%d" % sl, xTt[sl], xT_d[:, t0:t0 + 512].rearrange("(c p) n -> p c n", p=128),
                      writes=["xTt%d" % sl])
                for j in range(13):
                    bank = jglob % 6
                    jglob += 1
                    col = j * 128

                    def mm(e, bank=bank, col=col, sl=sl):
                        ins = None
                        for k in range(8):
                            ins = e.matmul(ps[:, bank, :], lhsT=w_bf[:, k, col:col + 128], rhs=xTt[sl][:, k, :],
                                           start=(k == 0), stop=(k == 7))
                        return ins
                    P.op("pe", mm, reads=["w_bf", "xTt%d" % sl], writes=["psA%d" % bank])
                    src = ps[:, bank, :]
                    if j < 4:
                        P.op("act", lambda e, src=src, j=j, sl=sl: e.activation(out=xr_s[sl][:, j, :], in_=src, func=AF.Copy),
                             reads=["psA%d" % bank], writes=["xr_s%d" % sl])
                    elif j < 8:
                        P.op("act", lambda e, src=src, j=j, sl=sl: e.activation(out=gy_s[sl][:, j - 4, :], in_=src, func=AF.Gelu),
                             reads=["psA%d" % bank], writes=["gy_s%d" % sl])
                    elif j < 12:
                        P.op("dve", lambda e, src=src, j=j, sl=sl: e.tensor_scalar(out=q_s[sl][:, j - 8, :], in0=src, scalar1=0.125,
                                                                                  scalar2=None, op0=ALU.mult),
                             reads=["psA%d" % bank], writes=["q_s%d" % sl])
                    else:
                        P.op("dve", lambda e, src=src, sl=sl: e.tensor_copy(out=k_s[sl], in_=src),
                             reads=["psA%d" % bank], writes=["k_s%d" % sl])
                vb = 6 + gi % 2

                def mmv(e, vb=vb, sl=sl):
                    ins = None
                    for s in range(4):
                        for k in range(8):
                            ins = e.matmul(ps[:, vb, s * 128:(s + 1) * 128], lhsT=xTt[sl][:, k, s * 128:(s + 1) * 128],
                                           rhs=w_bf[:, k, 1664:1792], start=(k == 0), stop=(k == 7))
                    return ins
                P.op("pe", mmv, reads=["w_bf", "xTt%d" % sl], writes=["psA%d" % vb])
                P.op("dve", lambda e, vb=vb, sl=sl: e.tensor_copy(out=v_s[sl], in_=ps[:, vb, :].rearrange("p (s c) -> p s c", s=4)),
                     reads=["psA%d" % vb], writes=["v_s%d" % sl])
                fm = lambda d_ap: d_ap[:, t0:t0 + 512].rearrange("(c p) n -> p c n", p=128)
                P.dma("pool", "sxr%d" % sl, fm(xrT_d), xr_s[sl], reads=["xr_s%d" % sl])
                P.dma("pool", "sgy%d" % sl, fm(gyT_d), gy_s[sl], reads=["gy_s%d" % sl])
                P.dma("pool", "sq%d" % sl, fm(qT_d), q_s[sl], reads=["q_s%d" % sl])
                P.dma("pool", "sk%d" % sl, kT_d[:, t0:t0 + 512], k_s[sl], reads=["k_s%d" % sl])
                P.dma("pool", "sv%d" % sl, v_d[t0:t0 + 512, :].rearrange("(s p) c -> p s c", p=128), v_s[sl],
                      reads=["v_s%d" % sl])
            P.barrier()
            A.top = m

        def chanvec(dst, src1d, sem, key):
            P.dma("sp", sem, dst, src1d.rearrange("(c p) -> p c", p=128), writes=[key], allow_slow_non_contiguous=True)

        def rg_setup(l, dr):
            R = {}
            wg32 = A.f32(8, 128)
            R["wg"] = A.bf16(8, 128)
            R["cw"] = A.f32(4, 4)
            R["cb"] = A.f32(4)
            R["ba"] = A.f32(4)
            R["bx"] = A.f32(4)
            lam = A.f32(4)
            R["cneg"] = A.f32(4)
            R["cneg2"] = A.f32(4)
            R["state"] = A.f32(4)
            t1 = A.f32(4)
            t2 = A.f32(4)
            P.op("pool", lambda e: e.memset(wg32, 0.0), writes=["wg32"])
            for gate, wd_ in enumerate((rg_wa_d, rg_wx_d)):
                srcv = wd_[l, dr].rearrange("(cc two) c o -> two c cc o", two=2)
                P.dma("sp", "rgw%d" % gate, wg32[0:64, gate * 4:(gate + 1) * 4, 0:64], srcv[0],
                      reads=["wg32"], writes=["wg32"])
                P.dma("sp", "rgw%d" % (2 + gate), wg32[64:128, gate * 4:(gate + 1) * 4, 64:128], srcv[1],
                      reads=["wg32"], writes=["wg32"])
            P.op("dve", lambda e: e.tensor_copy(out=R["wg"], in_=wg32), reads=["wg32"], writes=["wg"])
            for k in range(4):
                chanvec(R["cw"][:, k, :], rg_conv_w_d[l, k], "rgc%d" % k, "cw")
            chanvec(R["cb"], rg_conv_b_d[l], "rgc4", "cb")
            chanvec(R["ba"], rg_ba_d[l, dr], "rgc5", "ba")
            chanvec(R["bx"], rg_bx_d[l, dr], "rgc6", "bx")
            chanvec(lam, rg_lambda_d[l, dr], "rgc7", "lam")
            P.op("act", lambda e: e.activation(out=t1, in_=lam, func=AF.Exp, scale=-1.0), reads=["lam"], writes=["t1"])
            P.op("dve", lambda e: e.tensor_scalar(out=t2, in0=t1, scalar1=1.0, scalar2=None, op0=ALU.add), reads=["t1"], writes=["t2"])
            P.op("act", lambda e: e.activation(out=lam, in_=t2, func=AF.Ln), reads=["t2"], writes=["lam"])
            P.op("dve", lambda e: e.tensor_scalar(out=t2, in0=t2, scalar1=-1.0, scalar2=1e-30, op0=ALU.add, op1=ALU.add),
                 reads=["t2"], writes=["t2"])
            P.op("dve", lambda e: e.tensor_scalar(out=lam, in0=lam, scalar1=1e-30, scalar2=None, op0=ALU.add), reads=["lam"], writes=["lam"])
            P.op("dve", lambda e: e.reciprocal(out=t2, in_=t2), reads=["t2"], writes=["t2"])
            P.op("dve", lambda e: e.tensor_tensor(out=lam, in0=lam, in1=t2, op=ALU.mult), reads=["lam", "t2"], writes=["lam"])
            P.op("dve", lambda e: e.tensor_tensor(out=lam, in0=lam, in1=t1, op=ALU.mult), reads=["lam", "t1"], writes=["lam"])
            P.op("dve", lambda e: e.tensor_scalar(out=R["cneg"], in0=lam, scalar1=-8.0, scalar2=None, op0=ALU.mult), reads=["lam"], writes=["cneg"])
            P.op("dve", lambda e: e.tensor_scalar(out=R["cneg2"], in0=lam, scalar1=-16.0, scalar2=None, op0=ALU.mult), reads=["lam"], writes=["cneg2"])
            R["bufs"] = []
            for _ in range(2):
                B = {}
                B["xrw"] = A.f32(515)
                B["u"] = A.f32(512)
                B["ubf"] = A.bf16(512)
                B["r"] = A.f32(512)
                B["i"] = A.f32(512)
                B["a"] = A.f32(512)
                B["s"] = A.f32(512)
                B["b"] = A.f32(512)
                R["bufs"].append(B)
            R["it"] = 0
            return R

        def rg_chunk(R, dr, cc, t0, s0, s1, hout, hkey, first):
            it = R["it"]
            R["it"] += 1
            sl = it % 2
            B = R["bufs"][sl]
            kx = "rgx%d" % sl
            lo = max(t0 - 2, s0)
            hi = min(t0 + 513, s1)
            if lo > t0 - 2:
                P.op("pool", lambda e, B=B: e.memset(B["xrw"][:, 0:2], 0.0), writes=[kx])
            if hi < t0 + 513:
                P.op("pool", lambda e, B=B: e.memset(B["xrw"][:, 514:515], 0.0), writes=[kx])
            P.dma("sp", kx, B["xrw"][:, lo - (t0 - 2):hi - (t0 - 2)], xrT_d[cc * 128:(cc + 1) * 128, lo:hi],
                  reads=[kx], writes=[kx])
            cw, cb = R["cw"], R["cb"]
            ku = "rgu%d" % sl
            P.op("dve", lambda e, B=B: e.tensor_scalar(out=B["u"], in0=B["xrw"][:, 0:512], scalar1=cw[:, 0, cc:cc + 1],
                                                      scalar2=cb[:, cc:cc + 1], op0=ALU.mult, op1=ALU.add),
                 reads=[kx, "cw", "cb"], writes=[ku])
            for k in range(1, 4):
                P.op("dve", lambda e, B=B, k=k: e.scalar_tensor_tensor(out=B["u"], in0=B["xrw"][:, k:k + 512],
                                                                      scalar=cw[:, k, cc:cc + 1], in1=B["u"],
                                                                      op0=ALU.mult, op1=ALU.add),
                     reads=[kx, ku, "cw"], writes=[ku])
            P.op("act", lambda e, B=B: e.activation(out=B["ubf"], in_=B["u"], func=AF.Copy), reads=[ku], writes=["rgub%d" % sl])
            P.op("pe", lambda e, B=B: e.matmul(ps[:, 0, :], lhsT=R["wg"][:, cc, :], rhs=B["ubf"], start=True, stop=True),
                 reads=["rgub%d" % sl, "wg"], writes=["psga"])
            P.op("pe", lambda e, B=B: e.matmul(ps[:, 1, :], lhsT=R["wg"][:, 4 + cc, :], rhs=B["ubf"], start=True, stop=True),
                 reads=["rgub%d" % sl, "wg"], writes=["psgx"])
            P.op("act", lambda e, B=B: e.activation(out=B["r"], in_=ps[:, 0, :], func=AF.Sigmoid, bias=R["ba"][:, cc:cc + 1]),
                 reads=["psga", "ba"], writes=["rgr%d" % sl])
            P.op("act", lambda e, B=B: e.activation(out=B["i"], in_=ps[:, 1, :], func=AF.Sigmoid, bias=R["bx"][:, cc:cc + 1]),
                 reads=["psgx", "bx"], writes=["rgi%d" % sl])
            P.op("act", lambda e, B=B: e.activation(out=B["a"], in_=B["r"], func=AF.Exp, scale=R["cneg"][:, cc:cc + 1]),
                 reads=["rgr%d" % sl, "cneg"], writes=["rga%d" % sl])
            P.op("act", lambda e, B=B: e.activation(out=B["s"], in_=B["r"], func=AF.Exp, scale=R["cneg2"][:, cc:cc + 1]),
                 reads=["rgr%d" % sl, "cneg2"], writes=["rgs%d" % sl])
            P.op("dve", lambda e, B=B: e.tensor_scalar(out=B["s"], in0=B["s"], scalar1=1.0, scalar2=None, op0=ALU.min),
                 reads=["rgs%d" % sl], writes=["rgs%d" % sl])
            P.op("act", lambda e, B=B: e.activation(out=B["s"], in_=B["s"], func=AF.Sqrt, scale=-1.0, bias=1.0),
                 reads=["rgs%d" % sl], writes=["rgs%d" % sl])
            P.op("pool", lambda e, B=B: e.tensor_tensor(out=B["b"], in0=B["s"], in1=B["i"], op=ALU.mult),
                 reads=["rgs%d" % sl, "rgi%d" % sl], writes=["rgb%d" % sl])
            P.op("dve", lambda e, B=B: e.tensor_tensor(out=B["b"], in0=B["b"], in1=B["u"], op=ALU.mult),
                 reads=["rgb%d" % sl, ku], writes=["rgb%d" % sl])
            stc = R["state"][:, cc:cc + 1]
            if first:
                P.op("pool", lambda e: e.memset(stc, 0.0), writes=["rgst%d" % cc])
            if dr == 0:
                P.op("dve", lambda e, B=B: e.tensor_tensor_scan(out=hout, data0=B["a"], data1=B["b"], initial=stc,
                                                               op0=ALU.mult, op1=ALU.add),
                     reads=["rga%d" % sl, "rgb%d" % sl, "rgst%d" % cc], writes=[hkey])
                P.op("pool", lambda e: e.tensor_copy(out=stc, in_=hout[:, 511:512]), reads=[hkey], writes=["rgst%d" % cc])
            else:
                P.op("dve", lambda e, B=B: e.tensor_tensor_scan(out=hout[:, ::-1], data0=B["a"][:, ::-1], data1=B["b"][:, ::-1],
                                                               initial=stc, op0=ALU.mult, op1=ALU.add),
                     reads=["rga%d" % sl, "rgb%d" % sl, "rgst%d" % cc], writes=[hkey])
                P.op("pool", lambda e: e.tensor_copy(out=stc, in_=hout[:, 0:1]), reads=[hkey], writes=["rgst%d" % cc])

        def stageB(l):
            m = A.top
            R = rg_setup(l, 1)
            hst = [A.f32(4, 512) for _ in range(2)]
            wi = 0
            for (s0, n) in seqs:
                s1 = s0 + n
                for t0 in range(s1 - 512, s0 - 1, -512):
                    sl = wi % 2
                    wi += 1
                    for cc in range(4):
                        rg_chunk(R, 1, cc, t0, s0, s1, hst[sl][:, cc, :], "hst%d_%d" % (sl, cc), first=(t0 == s1 - 512))
                    P.dma("pool", "shb%d" % sl, hbT_d[:, t0:t0 + 512].rearrange("(c p) n -> p c n", p=128), hst[sl],
                          reads=["hst%d_%d" % (sl, cc) for cc in range(4)])
            P.barrier()
            A.top = m

        def layer_norm(y, ykey, g_bc, b_bc, gkey, st6, mv, rstd, tag):
            for hh in range(2):
                P.op("dve", lambda e, hh=hh: e.bn_stats(out=st6[:, hh * 6:(hh + 1) * 6], in_=y[:, hh * 512:(hh + 1) * 512]),
                     reads=[ykey], writes=[tag + "st%d" % hh])
            P.op("dve", lambda e: e.bn_aggr(out=mv, in_=st6), reads=[tag + "st0", tag + "st1"], writes=[tag + "mv"])
            P.op("act", lambda e: e.activation(out=rstd, in_=mv[:, 1:2], func=AF.Sqrt, bias=eps_ln, scale=1.0),
                 reads=[tag + "mv", "eps"], writes=[tag + "rs"])
            P.op("dve", lambda e: e.reciprocal(out=rstd, in_=rstd), reads=[tag + "rs"], writes=[tag + "rs"])
            P.op("dve", lambda e: e.tensor_scalar(out=y, in0=y, scalar1=mv[:, 0:1], scalar2=rstd, op0=ALU.subtract, op1=ALU.mult),
                 reads=[ykey, tag + "mv", tag + "rs"], writes=[ykey])
            P.op("pool", lambda e: e.tensor_tensor(out=y, in0=y, in1=g_bc, op=ALU.mult), reads=[ykey, gkey], writes=[ykey])
            P.op("pool", lambda e: e.tensor_tensor(out=y, in0=y, in1=b_bc, op=ALU.add), reads=[ykey, gkey], writes=[ykey])

        def stageC(l):
            m = A.top
            R = rg_setup(l, 0)
            wo = A.bf16(8, D)
            gcol = A.f32(8)
            chanvec(gcol[:, 0:4], norm_rg_g_d[l], "cg0", "gcol")
            chanvec(gcol[:, 4:8], norm_attn_g_d[l], "cg1", "gcol")
            stg = [A.f32(D) for _ in range(2)]
            for k in range(8):
                sl = k % 2
                P.dma("sp", "wst%d" % sl, stg[sl], w_out_d[l, k * 128:(k + 1) * 128, :], writes=["wst%d" % sl])
                P.op("dve", lambda e, k=k, sl=sl: e.tensor_scalar(out=wo[:, k, :], in0=stg[sl], scalar1=gcol[:, k:k + 1], scalar2=None,
                                                              op0=ALU.mult),
                     reads=["wst%d" % sl, "gcol"], writes=["wo"])
            g_bc = A.f32(D)
            b_bc = A.f32(D)
            P.dma("sp", "cg2", g_bc, ln1_g_d[l].partition_broadcast(128), writes=["lng"])
            P.dma("sp", "cg3", b_bc, ln1_b_d[l].partition_broadcast(128), writes=["lng"])
            hb_s = [A.f32(4, 512) for _ in range(2)]
            gy_s = [A.f32(4, 512) for _ in range(2)]
            hf = [A.f32(512) for _ in range(2)]
            rg32 = [A.f32(512) for _ in range(2)]
            sq_bf = A.bf16(4, 512)
            rg_bf = A.bf16(4, 512)
            at_bf = A.bf16(4, 512)
            qt = [[A.bf16(4, 512) for _ in range(2)] for _ in range(2)]
            kd = [A.bf16(2, 768) for _ in range(2)]
            va = [A.bf16(6, 2, 96) for _ in range(2)]
            PT = [A.bf16(512) for _ in range(6)]
            den = A.f32(8)
            o_bf = A.bf16(512)
            junk = A.bf16(512)
            ssa = A.f32(4)
            rs_rg = A.f32(4)
            rs_at = A.f32(4)
            xt = [A.f32(D) for _ in range(2)]
            yb = [A.f32(D) for _ in range(2)]
            x1bf = A.bf16(D)
            x1Ts = [A.bf16(8, 512) for _ in range(1)]
            st6 = A.f32(12)
            mv = A.f32(2)
            rstd = A.f32(1)
            for sl in range(2):
                P.op("pool", lambda e, sl=sl: e.memset(va[sl][:, :, :, 64:65], 1.0), writes=["va%d" % sl])
                P.op("pool", lambda e, sl=sl: e.memset(qt[sl][0][64:128, :, :], 0.0), writes=["qt%d" % sl])
                P.op("pool", lambda e, sl=sl: e.memset(qt[sl][1][0:64, :, :], 0.0), writes=["qt%d" % sl])
            po = ps[:, 3:5, 0:260].rearrange("p b (h e) -> p b h e", h=4)
            ps_ss = ps[:, 3, 384:388]
            pT = psbf(5)
            wi = 0
            ptc = 0
            sub_i = 0
            for (s0, n) in seqs:
                s1 = s0 + n
                nb = n // 128
                for t0 in range(s0, s1, 512):
                    sl = wi % 2
                    wi += 1
                    fm = lambda d_ap: d_ap[:, t0:t0 + 512].rearrange("(c p) n -> p c n", p=128)
                    P.dma("sp", "hbs%d" % sl, hb_s[sl], fm(hbT_d), writes=["hbs%d" % sl])
                    P.dma("sp", "gys%d" % sl, gy_s[sl], fm(gyT_d), writes=["gys%d" % sl])
                    P.dma("sp", "qt%d" % sl, qt[sl][0][0:64, :, :], fm(qT_d)[0:64], reads=["qt%d" % sl], writes=["qt%d" % sl])
                    P.dma("sp", "qt%d" % sl, qt[sl][1][64:128, :, :], fm(qT_d)[64:128], reads=["qt%d" % sl], writes=["qt%d" % sl])
                    klo = max(t0 - 128, s0)
                    khi = min(t0 + 640, s1)
                    o0 = klo - (t0 - 128)
                    for kvh in range(2):
                        for half in range(2):
                            P.dma("sp", "kd%d" % sl, kd[sl][half * 64:(half + 1) * 64, kvh, o0:o0 + khi - klo],
                                  kT_d[kvh * 64:(kvh + 1) * 64, klo:khi], writes=["kd%d" % sl])
                    for kvh in range(2):
                        P.dma("sp", "va%d" % sl, va[sl][:, o0 // 128:(o0 + khi - klo) // 128, kvh, 0:64],
                              v_d[klo:khi, kvh * 64:(kvh + 1) * 64].rearrange("(s p) d -> p s d", p=128),
                              reads=["va%d" % sl], writes=["va%d" % sl])
                    for cc in range(4):
                        hsl = (wi * 4 + cc) % 2
                        rg_chunk(R, 0, cc, t0, s0, s1, hf[hsl], "hf%d" % hsl, first=(t0 == s0))
                        P.op("pool", lambda e, hsl=hsl, cc=cc, sl=sl: e.tensor_tensor(out=rg32[hsl], in0=hf[hsl], in1=hb_s[sl][:, cc, :], op=ALU.add),
                             reads=["hf%d" % hsl, "hbs%d" % sl], writes=["rg32_%d" % hsl])
                        P.op("dve", lambda e, hsl=hsl, cc=cc, sl=sl: e.tensor_tensor(out=rg32[hsl], in0=rg32[hsl], in1=gy_s[sl][:, cc, :], op=ALU.mult),
                             reads=["rg32_%d" % hsl, "gys%d" % sl], writes=["rg32_%d" % hsl])
                        P.op("act", lambda e, hsl=hsl, cc=cc: e.activation(out=sq_bf[:, cc, :], in_=rg32[hsl], func=AF.Square),
                             reads=["rg32_%d" % hsl], writes=["sq%d" % cc])
                        P.op("pool", lambda e, hsl=hsl, cc=cc: e.tensor_copy(out=rg_bf[:, cc, :], in_=rg32[hsl]),
                             reads=["rg32_%d" % hsl], writes=["rgbf%d" % cc])

                    def mss(e):
                        ins = None
                        for s in range(4):
                            for cc in range(4):
                                ins = e.matmul(ps_ss[:, s:s + 1], lhsT=sq_bf[:, cc, s * 128:(s + 1) * 128], rhs=ones_bf[:, 0:1],
                                               start=(cc == 0), stop=(cc == 3))
                        return ins
                    P.op("pe", mss, reads=["sq%d" % c for c in range(4)] + ["ones"], writes=["psss"])
                    P.op("act", lambda e: e.activation(out=rs_rg, in_=ps_ss, func=AF.Sqrt, scale=1.0 / 512, bias=eps_rms),
                         reads=["psss", "eps"], writes=["rs_rg"])
                    P.op("dve", lambda e: e.reciprocal(out=rs_rg, in_=rs_rg), reads=["rs_rg"], writes=["rs_rg"])
                    if _CUT < 0.3:
                        continue
                    for j in range(4):
                        jb = (t0 - s0) // 128 + j
                        rs_valid = [r for r in range(3) if 0 <= jb + r - 1 < nb]
                        for hg in range(2):
                            pts = {}
                            for r in rs_valid:
                                kc = (j + r) * 128
                                pt = PT[ptc % 6]
                                ptk = "PT%d" % (ptc % 6)
                                ptc += 1
                                pts[r] = (pt, ptk)

                                def mqk(e, r=r, hg=hg, kc=kc, j=j, sl=sl):
                                    ins = e.matmul(ps[:, 2, :], lhsT=ident, rhs=biasT[:, r, hg, :], start=True, stop=(_CUT < 0.6))
                                    for h4 in range(4 if _CUT >= 0.6 else 0):
                                        head = hg * 4 + h4
                                        cq, half = head // 2, head % 2
                                        ins = e.matmul(ps[:, 2, h4 * 128:(h4 + 1) * 128], lhsT=kd[sl][:, hg, kc:kc + 128],
                                                       rhs=qt[sl][half][:, cq, j * 128:(j + 1) * 128], start=False, stop=(h4 == 3))
                                    return ins
                                P.op("pe", mqk, reads=["ident", "biasT", "kd%d" % sl, "qt%d" % sl], writes=["psS"])
                                P.op("act", lambda e, pt=pt: e.activation(out=pt, in_=ps[:, 2, :], func=AF.Exp),
                                     reads=["psS"], writes=[ptk])

                            def mpv(e, hg=hg, j=j, sl=sl, pts=pts, rs_valid=rs_valid):
                                ins = None
                                for h4 in range(4):
                                    for r in rs_valid:
                                        ins = e.matmul(po[:, hg, h4, :], lhsT=pts[r][0][:, h4 * 128:(h4 + 1) * 128],
                                                       rhs=va[sl][:, j + r, hg, 0:65], start=(r == rs_valid[0]), stop=(r == rs_valid[-1]))
                                return ins
                            if _CUT >= 1:
                                P.op("pe", mpv, reads=[pts[r][1] for r in rs_valid] + ["va%d" % sl], writes=["po%d" % hg])
                        if _CUT < 2:
                            continue
                        P.op("dve", lambda e: e.tensor_tensor(out=den.rearrange("p (b h) -> p b h", b=2), in0=po[:, :, :, 64],
                                                              in1=esink[:, l, :].rearrange("p (b h) -> p b h", b=2), op=ALU.add),
                             reads=["po0", "po1", "esink"], writes=["den"])
                        P.op("dve", lambda e: e.reciprocal(out=den, in_=den), reads=["den"], writes=["den"])
                        P.op("dve", lambda e: e.tensor_tensor(out=o_bf.rearrange("p (b h d) -> p b h d", b=2, h=4), in0=po[:, :, :, 0:64],
                                                              in1=den.rearrange("p (b h) -> p b h", b=2).unsqueeze(3).to_broadcast([128, 2, 4, 64]),
                                                              op=ALU.mult),
                             reads=["po0", "po1", "den"], writes=["o_bf"])
                        P.op("pool", lambda e, j=j: e.memset(ssa[:, j:j + 1], 0.0), writes=["ssa%d" % j])
                        P.op("act", lambda e, j=j: e.activation(out=junk, in_=o_bf, func=AF.Square, accum_out=ssa[:, j:j + 1]),
                             reads=["o_bf"], writes=["junk", "ssa%d" % j])
                        P.op("act", lambda e, j=j: e.activation(out=rs_at[:, j:j + 1], in_=ssa[:, j:j + 1], func=AF.Sqrt, scale=1.0 / 512, bias=eps_rms),
                             reads=["ssa%d" % j, "eps"], writes=["rs_at%d" % j])
                        P.op("dve", lambda e, j=j: e.reciprocal(out=rs_at[:, j:j + 1], in_=rs_at[:, j:j + 1]), reads=["rs_at%d" % j], writes=["rs_at%d" % j])
                        P.op("dve", lambda e, j=j: e.tensor_scalar(out=o_bf, in0=o_bf, scalar1=rs_at[:, j:j + 1], scalar2=None, op0=ALU.mult),
                             reads=["o_bf", "rs_at%d" % j], writes=["o_bf"])

                        def mtr(e):
                            ins = None
                            for c in range(4):
                                ins = e.transpose(out=pT[:, c, :], in_=o_bf[:, c * 128:(c + 1) * 128], identity=ident)
                            return ins
                        P.op("pe", mtr, reads=["o_bf", "ident"], writes=["psT"])
                        P.op("dve", lambda e, j=j: e.tensor_copy(out=at_bf[:, :, j * 128:(j + 1) * 128], in_=pT[:, 0:4, :]),
                             reads=["psT"], writes=["atbf%d" % j])
                    if _CUT < 3:
                        continue
                    for s in range(4):
                        xs = sub_i % 2
                        sub_i += 1
                        tt = t0 + s * 128
                        src_x = x_d if l == 0 else xres_d
                        P.dma("sp", "xt%d" % xs, xt[xs], src_x[tt:tt + 128, :], writes=["xt%d" % xs])
                        P.op("act", lambda e, xs=xs: e.activation(out=xt[xs], in_=xt[xs], func=AF.Copy, scale=ALPHA),
                             reads=["xt%d" % xs], writes=["xt%d" % xs])

                        def mwo(e, s=s, src=rg_bf, kb=0):
                            ins = None
                            for hh in range(2):
                                for c in range(4):
                                    ins = e.matmul(ps[:, 6 + hh, :], lhsT=src[:, c, s * 128:(s + 1) * 128],
                                                   rhs=wo[:, kb + c, hh * 512:(hh + 1) * 512], start=(c == 0), stop=(c == 3))
                            return ins
                        P.op("pe", mwo, reads=["rgbf%d" % c for c in range(4)] + ["wo"], writes=["psW"])
                        pw = ps[:, 6:8, :]
                        yv = yb[xs].rearrange("p (b n) -> p b n", b=2)
                        xv = xt[xs].rearrange("p (b n) -> p b n", b=2)
                        P.op("dve", lambda e, s=s, yv=yv, xv=xv: e.scalar_tensor_tensor(out=yv, in0=pw, scalar=rs_rg[:, s:s + 1], in1=xv,
                                                                                      op0=ALU.mult, op1=ALU.add),
                             reads=["psW", "rs_rg", "xt%d" % xs], writes=["yb%d" % xs])
                        P.op("pe", lambda e, s=s: mwo(e, s=s, src=at_bf, kb=4), reads=["atbf%d" % s, "wo"], writes=["psW"])
                        P.op("dve", lambda e, yv=yv: e.tensor_tensor(out=yv, in0=pw, in1=yv, op=ALU.add),
                             reads=["psW", "yb%d" % xs], writes=["yb%d" % xs])
                        layer_norm(yb[xs], "yb%d" % xs, g_bc, b_bc, "lng", st6, mv, rstd, "ln")
                        P.dma("pool", "sx1_%d" % xs, x1_d[tt:tt + 128, :], yb[xs], reads=["yb%d" % xs])
                        P.op("act", lambda e, xs=xs: e.activation(out=x1bf, in_=yb[xs], func=AF.Copy), reads=["yb%d" % xs], writes=["x1bf"])
                        transpose_store(x1bf, 5, x1Ts[0], s * 128, "x1bf", "psT", "x1Ts0", "act")
                    P.dma("pool", "sx1T0", x1T_d[:, t0:t0 + 512].rearrange("(c p) n -> p c n", p=128), x1Ts[0],
                          reads=["x1Ts0"])
            P.barrier()
            A.top = m

        def stageD(l, last):
            m = A.top
            wgu = A.bf16(8, 2 * DFF)
            wd = A.bf16(NFF, D)
            fcw = A.f32(3, NFF)
            fcb = A.f32(NFF)
            g_bc = A.f32(D)
            b_bc = A.f32(D)
            m2 = A.top
            stg = [A.f32(1408) for _ in range(3)]
            ci = 0
            for k in range(8):
                for pc in range(4):
                    sl = ci % 3
                    P.dma("sp", "wst%d" % sl, stg[sl], ffn_w_in_d[l, k * 128:(k + 1) * 128, pc * 1408:(pc + 1) * 1408],
                          writes=["wst%d" % sl])
                    dst = wgu[:, k, pc * 1408:(pc + 1) * 1408]
                    if ci % 3 == 0:
                        P.op("act", lambda e, dst=dst, sl=sl: e.activation(out=dst, in_=stg[sl], func=AF.Copy), reads=["wst%d" % sl], writes=["wgu"])
                    elif ci % 3 == 1:
                        P.op("dve", lambda e, dst=dst, sl=sl: e.tensor_copy(out=dst, in_=stg[sl]), reads=["wst%d" % sl], writes=["wgu"])
                    else:
                        P.op("pool", lambda e, dst=dst, sl=sl: e.tensor_copy(out=dst, in_=stg[sl]), reads=["wst%d" % sl], writes=["wgu"])
                    ci += 1
            for c in range(NFF):
                sl = ci % 3
                P.dma("sp", "wst%d" % sl, stg[sl][:, 0:D], ffn_w_out_d[l, c * 128:(c + 1) * 128, :], writes=["wst%d" % sl])
                dst = wd[:, c, :]
                if ci % 3 == 0:
                    P.op("act", lambda e, dst=dst, sl=sl: e.activation(out=dst, in_=stg[sl][:, 0:D], func=AF.Copy), reads=["wst%d" % sl], writes=["wd"])
                elif ci % 3 == 1:
                    P.op("dve", lambda e, dst=dst, sl=sl: e.tensor_copy(out=dst, in_=stg[sl][:, 0:D]), reads=["wst%d" % sl], writes=["wd"])
                else:
                    P.op("pool", lambda e, dst=dst, sl=sl: e.tensor_copy(out=dst, in_=stg[sl][:, 0:D]), reads=["wst%d" % sl], writes=["wd"])
                ci += 1
            for k in range(3):
                chanvec(fcw[:, k, :], ffn_conv_w_d[l, k], "fc%d" % k, "fcw")
            chanvec(fcb, ffn_conv_b_d[l], "fc3", "fcb")
            P.dma("sp", "cg2", g_bc, ln2_g_d[l].partition_broadcast(128), writes=["lng"])
            P.dma("sp", "cg3", b_bc, ln2_b_d[l].partition_broadcast(128), writes=["lng"])
            P.barrier()
            A.top = m2
            TT = 256
            x1t = [A.bf16(8, TT + 2) for _ in range(2)]
            hT = A.bf16(NFF, TT)
            acc = [A.f32(TT) for _ in range(2)]
            gl = [A.f32(TT) for _ in range(2)]
            xt = [A.f32(D) for _ in range(1)]
            yb = [A.f32(D) for _ in range(2)]
            xbf = A.bf16(D)
            xTs = [A.bf16(8, TT) for _ in range(1)]
            st6 = A.f32(12)
            mv = A.f32(2)
            rstd = A.f32(1)
            ti = 0
            sub_i = 0
            cgl = 0
            for (s0, n) in seqs:
                s1 = s0 + n
                for t0 in range(s0, s1, TT):
                    sl = ti % 2
                    ti += 1
                    kx = "x1t%d" % sl
                    lo = max(t0 - 1, s0)
                    hi = min(t0 + TT + 1, s1)
                    if lo > t0 - 1:
                        P.op("pool", lambda e, sl=sl: e.memset(x1t[sl][:, :, 0:1], 0.0), writes=[kx])
                    if hi < t0 + TT + 1:
                        P.op("pool", lambda e, sl=sl: e.memset(x1t[sl][:, :, TT + 1:TT + 2], 0.0), writes=[kx])
                    P.dma("sp", kx, x1t[sl][:, :, lo - (t0 - 1):hi - (t0 - 1)],
                          x1T_d[:, lo:hi].rearrange("(c p) n -> p c n", p=128), reads=[kx], writes=[kx])
                    for c in range(NFF):
                        pb = cgl % 2
                        cgl += 1

                        def mg(e, c=c, pb=pb, sl=sl):
                            ins = None
                            for k in range(8):
                                ins = e.matmul(ps[:, pb, 0:TT + 2], lhsT=wgu[:, k, c * 128:(c + 1) * 128], rhs=x1t[sl][:, k, :],
                                               start=(k == 0), stop=(k == 7))
                            return ins

                        def mu(e, c=c, pb=pb, sl=sl):
                            ins = None
                            for k in range(8):
                                ins = e.matmul(ps[:, 2 + pb, 0:TT], lhsT=wgu[:, k, DFF + c * 128:DFF + (c + 1) * 128],
                                               rhs=x1t[sl][:, k, 1:TT + 1], start=(k == 0), stop=(k == 7))
                            return ins
                        P.op("pe", mg, reads=["wgu", kx], writes=["psg%d" % pb])
                        P.op("pe", mu, reads=["wgu", kx], writes=["psu%d" % pb])
                        ak = "acc%d" % pb
                        P.op("dve", lambda e, c=c, pb=pb: e.tensor_scalar(out=acc[pb], in0=ps[:, pb, 1:TT + 1], scalar1=fcw[:, 1, c:c + 1],
                                                                        scalar2=fcb[:, c:c + 1], op0=ALU.mult, op1=ALU.add),
                             reads=["psg%d" % pb, "fcw", "fcb"], writes=[ak])
                        P.op("dve", lambda e, c=c, pb=pb: e.scalar_tensor_tensor(out=acc[pb], in0=ps[:, pb, 0:TT], scalar=fcw[:, 0, c:c + 1],
                                                                               in1=acc[pb], op0=ALU.mult, op1=ALU.add),
                             reads=["psg%d" % pb, "fcw", ak], writes=[ak])
                        P.op("dve", lambda e, c=c, pb=pb: e.scalar_tensor_tensor(out=acc[pb], in0=ps[:, pb, 2:TT + 2], scalar=fcw[:, 2, c:c + 1],
                                                                               in1=acc[pb], op0=ALU.mult, op1=ALU.add),
                             reads=["psg%d" % pb, "fcw", ak], writes=[ak])
                        P.op("act", lambda e, pb=pb: e.activation(out=gl[pb], in_=acc[pb], func=AF.Gelu), reads=[ak], writes=["gl%d" % pb])
                        P.op("dve", lambda e, c=c, pb=pb: e.tensor_tensor(out=hT[:, c, :], in0=gl[pb], in1=ps[:, 2 + pb, 0:TT], op=ALU.mult),
                             reads=["gl%d" % pb, "psu%d" % pb], writes=["hT%d" % c])
                    for s in range(TT // 128):
                        xs = sub_i % 2
                        xq = 0
                        sub_i += 1
                        tt = t0 + s * 128

                        def mdn(e, s=s):
                            ins = None
                            for hh in range(2):
                                for c in range(NFF):
                                    ins = e.matmul(ps[:, 4 + 2 * (s % 2) + hh, :], lhsT=hT[:, c, s * 128:(s + 1) * 128],
                                                   rhs=wd[:, c, hh * 512:(hh + 1) * 512], start=(c == 0), stop=(c == NFF - 1))
                            return ins
                        P.op("pe", mdn, reads=["hT%d" % c for c in range(NFF)] + ["wd"], writes=["psd%d" % (s % 2)])
                        P.dma("sp", "xt%d" % xq, xt[xq], x1_d[tt:tt + 128, :], writes=["xt%d" % xq])
                        P.op("act", lambda e, xq=xq: e.activation(out=xt[xq], in_=xt[xq], func=AF.Copy, scale=ALPHA),
                             reads=["xt%d" % xq], writes=["xt%d" % xq])
                        b0 = 4 + 2 * (s % 2)
                        pw = ps[:, b0:b0 + 2, :]
                        yv = yb[xs].rearrange("p (b n) -> p b n", b=2)
                        xv = xt[xq].rearrange("p (b n) -> p b n", b=2)
                        P.op("dve", lambda e, pw=pw, yv=yv, xv=xv: e.tensor_tensor(out=yv, in0=pw, in1=xv, op=ALU.add),
                             reads=["psd%d" % (s % 2), "xt%d" % xq], writes=["yb%d" % xs])
                        layer_norm(yb[xs], "yb%d" % xs, g_bc, b_bc, "lng", st6, mv, rstd, "ln")
                        if last:
                            P.dma("pool", "sy%d" % xs, y_d[tt:tt + 128, :], yb[xs], reads=["yb%d" % xs])
                        else:
                            P.dma("pool", "sy%d" % xs, xres_d[tt:tt + 128, :], yb[xs], reads=["yb%d" % xs])
                            P.op("act", lambda e, xs=xs: e.activation(out=xbf, in_=yb[xs], func=AF.Copy), reads=["yb%d" % xs], writes=["xbf"])
                            transpose_store(xbf, s % 2, xTs[0], s * 128, "xbf", "psg%d" % (s % 2), "xTs0", "act")
                    if not last:
                        P.dma("pool", "sxT0", xT_d[:, t0:t0 + TT].rearrange("(c p) n -> p c n", p=128), xTs[0],
                              reads=["xTs0"])
            P.barrier()
            A.top = m

        stage0()
        for l in range(depth):
            stageA(l)
            stageB(l)
            stageC(l)
            stageD(l, l == depth - 1)
        P.emit()
    return nc


SEQS = [(0, 2048), (2048, 2048), (4096, 16384)]
_WNAMES = ["w_in", "rg_conv_w", "rg_conv_b", "rg_wa", "rg_ba", "rg_wx", "rg_bx", "rg_lambda", "attn_sink", "rel_bias",
           "norm_rg_g", "norm_attn_g", "w_out", "ln1_g", "ln1_b", "ffn_w_in", "ffn_conv_w", "ffn_conv_b", "ffn_w_out",
           "ln2_g", "ln2_b"]


def kernel(**inputs):
    x_prompt = np.asarray(inputs["x_prompt"], dtype=np.float32)
    x_sample = np.asarray(inputs["x_sample"], dtype=np.float32)
    depth = int(np.asarray(inputs["w_in"]).shape[0])
    nc = build_program(SEQS, depth)
    shared = {k: np.ascontiguousarray(np.asarray(inputs[k], dtype=np.float32)) for k in _WNAMES}
    shared["c_ident"] = np.eye(128, dtype=np.float32)
    shared["c_onehot"] = _band_onehot()
    in_maps = []
    for c in range(8):
        xs = np.concatenate([x_prompt[2 * c], x_prompt[2 * c + 1], x_sample[c % 2]], axis=0)
        mm = dict(shared)
        mm["x"] = np.ascontiguousarray(xs)
        in_maps.append(mm)
    res = run_bass_kernel_spmd(nc, in_maps, core_ids=list(range(8)))
    y_prompt = np.empty_like(x_prompt)
    y_sample = np.empty_like(x_sample)
    for c in range(8):
        y = np.asarray(res.results[c]["y"], dtype=np.float32)
        y_prompt[2 * c] = y[0:2048]
        y_prompt[2 * c + 1] = y[2048:4096]
        if c < 2:
            y_sample[c] = y[4096:20480]
    return (y_prompt, y_sample)
```

```python
import numpy as np
import concourse.bass as bass
import concourse.mybir as mybir
from concourse.bass_utils import run_bass_kernel_spmd

F32 = mybir.dt.float32
BF16 = mybir.dt.bfloat16
AF = mybir.ActivationFunctionType
ALU = mybir.AluOpType
AX = mybir.AxisListType

COMPUTE = ("pe", "act", "dve", "pool")


class _Op:
    __slots__ = ("eng", "fn", "deps", "signal", "val", "dma_sem", "kind", "name")

    def __init__(self, eng, fn, kind, name=""):
        self.eng = eng
        self.fn = fn
        self.deps = []
        self.signal = False
        self.val = None
        self.dma_sem = None
        self.kind = kind
        self.name = name


class Prog:
    def __init__(self, nc):
        self.nc = nc
        self.queues = {e: [] for e in ("pe", "act", "dve", "pool", "sp")}
        self.last_w = {}
        self.readers = {}
        self.dma_sem_names = []
        self.all_ops = []
        self._pending = {}

    def _deps(self, op, reads, writes):
        deps = []
        for k in reads:
            w = self.last_w.get(k)
            if w is not None:
                deps.append(w)
        for k in writes:
            w = self.last_w.get(k)
            if w is not None:
                deps.append(w)
            for r in self.readers.get(k, ()):
                deps.append(r)
        seen = set()
        for d in deps:
            if d is op or id(d) in seen:
                continue
            seen.add(id(d))
            if d.kind == "c" and op.kind == "c" and d.eng == op.eng and d.eng == "pe":
                continue
            op.deps.append(d)
            d.signal = True
        pend = self._pending.pop(op.eng, None)
        if pend:
            for d in pend:
                if d is op or id(d) in seen:
                    continue
                if d.kind == "c" and d.eng == op.eng and op.kind == "c" and d.eng == "pe":
                    continue
                seen.add(id(d))
                op.deps.append(d)
                d.signal = True
        for k in reads:
            self.readers.setdefault(k, []).append(op)
        for k in writes:
            self.last_w[k] = op
            self.readers[k] = []

    def op(self, eng, fn, reads=(), writes=(), name=""):
        o = _Op(eng, fn, "c", name)
        self._deps(o, reads, writes)
        self.queues[eng].append(o)
        self.all_ops.append(o)
        return o

    def dma(self, queue, sem, out, in_, reads=(), writes=(), **kw):
        if sem not in self.dma_sem_names:
            self.dma_sem_names.append(sem)
        o = _Op(queue, None, "d", sem)
        o.dma_sem = sem
        o.fn = lambda e, out=out, in_=in_, kw=kw: e.dma_start(out=out, in_=in_, **kw)
        self._deps(o, reads, writes)
        self.queues[queue].append(o)
        self.all_ops.append(o)
        return o

    def barrier(self):
        tails = []
        for q in self.queues.values():
            last_c = None
            for o in q:
                if o.kind == "c":
                    last_c = o
            if last_c is not None:
                tails.append(last_c)
        last_dma = {}
        for o in self.all_ops:
            if o.kind == "d":
                last_dma[o.dma_sem] = o
        tails.extend(last_dma.values())
        self.last_w = {}
        self.readers = {}
        for t in tails:
            t.signal = True
        self._pending = {e: list(tails) for e in self.queues}

    def emit(self):
        nc = self.nc
        engs = {"pe": nc.tensor, "act": nc.scalar, "dve": nc.vector, "pool": nc.gpsimd, "sp": nc.sync}
        cnt = {e: 0 for e in COMPUTE}
        dcnt = {s: 0 for s in self.dma_sem_names}
        for o in self.all_ops:
            if o.kind == "d":
                dcnt[o.dma_sem] += 16
                o.val = dcnt[o.dma_sem]
            elif o.signal:
                cnt[o.eng] += 1
                o.val = cnt[o.eng]
        import contextlib
        with contextlib.ExitStack() as st:
            csem = {e: st.enter_context(nc.semaphore("s_" + e)) for e in COMPUTE}
            dsem = {s: st.enter_context(nc.semaphore("d_" + s)) for s in self.dma_sem_names}
            block = st.enter_context(nc.Block())

            def semof(o):
                return dsem[o.dma_sem] if o.kind == "d" else csem[o.eng]

            def run_queue(ename, e):
                known = {}
                for o in self.queues[ename]:
                    need = {}
                    for d in o.deps:
                        s = ("d", d.dma_sem) if d.kind == "d" else ("c", d.eng)
                        if need.get(s, 0) < d.val:
                            need[s] = d.val
                    for s, v in need.items():
                        if known.get(s, 0) >= v:
                            continue
                        known[s] = v
                        e.wait_ge(dsem[s[1]] if s[0] == "d" else csem[s[1]], v)
                    ins = o.fn(e)
                    if o.kind == "d":
                        ins.then_inc(dsem[o.dma_sem], 16)
                    elif o.signal:
                        ins.then_inc(csem[o.eng], 1)
                if ename in final_waits:
                    for o in final_waits[ename]:
                        e.wait_ge(semof(o), o.val)

            last_dma = {}
            for o in self.all_ops:
                if o.kind == "d":
                    last_dma[o.dma_sem] = o
            final_waits = {"sp": list(last_dma.values())}

            @block.tensor
            def _(e):
                run_queue("pe", e)

            @block.scalar
            def _(e):
                run_queue("act", e)

            @block.vector
            def _(e):
                run_queue("dve", e)

            @block.gpsimd
            def _(e):
                run_queue("pool", e)

            @block.sync
            def _(e):
                run_queue("sp", e)


D = 1024
RGW = 512
INC = 1792
DFF = 2816
NFF = 22
ALPHA = float(8 ** 0.25)
LN_EPS = 1e-5
RMS_EPS = 1e-6
NEGB = -30000.0
_CUT = 99


def _band_onehot():
    half, exact = 16, 8
    r = np.arange(3)[:, None, None]
    q = np.arange(128)[None, :, None]
    k = np.arange(128)[None, None, :]
    rel = (r - 1) * 128 + k - q
    n = np.abs(rel)
    large = exact + (np.log(np.maximum(n, 1) / exact) / np.log(128 / exact) * (half - exact)).astype(np.int32)
    large = np.minimum(large, half - 1)
    bucket = np.where(n < exact, n, large) + (rel > 0).astype(np.int32) * half
    bucket = np.where(n <= 128, bucket, 32)
    oh = (bucket[None] == np.arange(33)[:, None, None, None]).astype(np.float32)
    return np.ascontiguousarray(oh.reshape(33, 3 * 128 * 128))


class Arena:
    def __init__(self, ap, nwords):
        self.ap = ap
        self.n = nwords
        self.top = 0

    def f32(self, *shape):
        n = int(np.prod(shape))
        off = self.top
        self.top += n
        assert self.top <= self.n, ("arena overflow", self.top)
        v = self.ap[:, off:off + n]
        return self._shape(v, shape)

    def bf16(self, *shape):
        n = int(np.prod(shape))
        w = (n + 1) // 2
        off = self.top
        self.top += w
        assert self.top <= self.n, ("arena overflow", self.top)
        v = self.ap[:, off:off + w].bitcast(BF16)[:, 0:n]
        return self._shape(v, shape)

    @staticmethod
    def _shape(v, shape):
        if len(shape) == 1:
            return v
        if len(shape) == 2:
            return v.rearrange("p (a b) -> p a b", a=shape[0])
        if len(shape) == 3:
            return v.rearrange("p (a b c) -> p a b c", a=shape[0], b=shape[1])
        raise ValueError(shape)


def build_program(seqs, depth, debug=False):
    NT = sum(n for _, n in seqs)
    nc = bass.Bass("TRN2", target_bir_lowering=False)

    def din(name, shape, dt=F32):
        return nc.dram_tensor(name, list(shape), dt, kind="ExternalInput").ap()

    skind = "ExternalOutput" if debug else "Internal"

    def dscr(name, shape, dt):
        return nc.dram_tensor(name, list(shape), dt, kind=skind).ap()

    x_d = din("x", [NT, D])
    w_in_d = din("w_in", [depth, D, INC])
    rg_conv_w_d = din("rg_conv_w", [depth, 4, RGW])
    rg_conv_b_d = din("rg_conv_b", [depth, RGW])
    rg_wa_d = din("rg_wa", [depth, 2, 8, 64, 64])
    rg_ba_d = din("rg_ba", [depth, 2, RGW])
    rg_wx_d = din("rg_wx", [depth, 2, 8, 64, 64])
    rg_bx_d = din("rg_bx", [depth, 2, RGW])
    rg_lambda_d = din("rg_lambda", [depth, 2, RGW])
    attn_sink_d = din("attn_sink", [depth, 8])
    rel_bias_d = din("rel_bias", [32, 8])
    norm_rg_g_d = din("norm_rg_g", [depth, RGW])
    norm_attn_g_d = din("norm_attn_g", [depth, RGW])
    w_out_d = din("w_out", [depth, D, D])
    ln1_g_d = din("ln1_g", [depth, D])
    ln1_b_d = din("ln1_b", [depth, D])
    ffn_w_in_d = din("ffn_w_in", [depth, D, 2 * DFF])
    ffn_conv_w_d = din("ffn_conv_w", [depth, 3, DFF])
    ffn_conv_b_d = din("ffn_conv_b", [depth, DFF])
    ffn_w_out_d = din("ffn_w_out", [depth, DFF, D])
    ln2_g_d = din("ln2_g", [depth, D])
    ln2_b_d = din("ln2_b", [depth, D])
    ident_d = din("c_ident", [128, 128])
    oh_d = din("c_onehot", [33, 3 * 128 * 128])

    y_d = nc.dram_tensor("y", [NT, D], F32, kind="ExternalOutput").ap()

    xT_d = dscr("s_xT", [D, NT], BF16)
    xres_d = dscr("s_xres", [NT, D], F32)
    xrT_d = dscr("s_xrT", [RGW, NT], F32)
    gyT_d = dscr("s_gyT", [RGW, NT], F32)
    qT_d = dscr("s_qT", [RGW, NT], BF16)
    kT_d = dscr("s_kT", [128, NT], BF16)
    v_d = dscr("s_v", [NT, 128], BF16)
    hbT_d = dscr("s_hbT", [RGW, NT], F32)
    x1_d = dscr("s_x1", [NT, D], F32)
    x1T_d = dscr("s_x1T", [D, NT], BF16)

    P = Prog(nc)
    import contextlib
    with contextlib.ExitStack() as st:
        ar_t = st.enter_context(nc.sbuf_tensor("arena", [128, 49152], F32))
        ps = st.enter_context(nc.psum_tensor("ps", [128, 8, 512], F32))
        A = Arena(ar_t[:, :], 49152)

        def psbf(bank):
            return ps[:, bank, :].bitcast(BF16).rearrange("p (c n) -> p c n", c=8)

        ident = A.bf16(128)
        ones_bf = A.bf16(2)
        biasT = A.bf16(3, 2, 512)
        esink = A.f32(depth, 8)
        rb_bf = A.bf16(8)
        eps_ln = A.f32(1)
        eps_rms = A.f32(1)
        quart_t = A.f32(1)
        base_mark = A.top
        P.op("pool", lambda e: e.memset(quart_t, 0.25), writes=["eps"])
        P.op("pool", lambda e: e.memset(eps_ln, LN_EPS), writes=["eps"])
        P.op("pool", lambda e: e.memset(eps_rms, RMS_EPS), writes=["eps"])

        tmp_id = A.f32(128)
        P.dma("sp", "c0", tmp_id, ident_d, writes=["tmp_id"])
        P.op("dve", lambda e: e.tensor_copy(out=ident, in_=tmp_id), reads=["tmp_id"], writes=["ident"])
        P.op("pool", lambda e: e.memset(ones_bf, 1.0), writes=["ones"])
        P.dma("sp", "c1", esink.rearrange("p l h -> p (l h)"),
              attn_sink_d.rearrange("l h -> (l h)").partition_broadcast(128), writes=["esink"])
        P.op("act", lambda e: e.activation(out=esink, in_=esink, func=AF.Exp), reads=["esink"], writes=["esink"])
        rb32 = A.f32(8)
        P.op("pool", lambda e: e.memset(rb32[0:64, :], NEGB), writes=["rb32"])
        P.dma("sp", "c2", rb32[0:32, :], rel_bias_d, reads=["rb32"], writes=["rb32"])
        P.op("dve", lambda e: e.tensor_copy(out=rb_bf[0:64, :], in_=rb32[0:64, :]), reads=["rb32"], writes=["rb_bf"])
        oh32 = A.f32(128 * 128)
        ohbf = A.bf16(128 * 128)
        for r in range(3):
            P.dma("sp", "c3", oh32[0:33, :], oh_d[:, r * 16384:(r + 1) * 16384], reads=["oh32"], writes=["oh32"])
            P.op("act", lambda e: e.activation(out=ohbf[0:33, :], in_=oh32[0:33, :], func=AF.Copy),
                 reads=["oh32"], writes=["ohbf"])
            for hb in range(2):
                def mm(e, hb=hb):
                    ins = None
                    for qq in range(64):
                        q = hb * 64 + qq
                        ins = e.matmul(ps[:, hb, qq * 8:(qq + 1) * 8], lhsT=ohbf[0:33, q * 128:(q + 1) * 128],
                                       rhs=rb_bf[0:33, :], start=True, stop=True)
                    return ins
                P.op("pe", mm, reads=["ohbf", "rb_bf"], writes=["psb%d" % hb])
            for hg in range(2):
                for hb in range(2):
                    src = ps[:, hb, :].rearrange("p (q h) -> p h q", h=8)[:, hg * 4:(hg + 1) * 4, :]
                    dst = biasT[:, r, hg, :].rearrange("p (h q) -> p h q", h=4)[:, :, hb * 64:(hb + 1) * 64]
                    P.op("dve", lambda e, src=src, dst=dst: e.tensor_copy(out=dst, in_=src),
                         reads=["psb%d" % hb], writes=["biasT"])
        P.barrier()
        A.top = base_mark

        tiles512 = []
        for (s0, n) in seqs:
            for t in range(s0, s0 + n, 512):
                tiles512.append((t, s0, s0 + n))

        def transpose_store(src_bf, psbank, dstT, col0, key_src, key_ps, key_dst, evac_eng):
            pv = psbf(psbank)

            def tr(e):
                ins = None
                for c in range(8):
                    ins = e.transpose(out=pv[:, c, :], in_=src_bf[:, c * 128:(c + 1) * 128], identity=ident)
                return ins
            P.op("pe", tr, reads=[key_src, "ident"], writes=[key_ps])
            if evac_eng == "act":
                P.op("act", lambda e: e.activation(out=dstT[:, :, col0:col0 + 128], in_=pv, func=AF.Copy),
                     reads=[key_ps], writes=[key_dst])
            else:
                P.op("dve", lambda e: e.tensor_copy(out=dstT[:, :, col0:col0 + 128], in_=pv),
                     reads=[key_ps], writes=[key_dst])

        def stage0():
            m = A.top
            xin = [A.f32(4, D) for _ in range(2)]
            xbf = [A.bf16(4, D) for _ in range(2)]
            xTs = [A.bf16(8, 512) for _ in range(2)]
            for gi, (t0, _, _) in enumerate(tiles512):
                sl = gi % 2
                P.dma("sp", "xin%d" % sl, xin[sl], x_d[t0:t0 + 512, :].rearrange("(s p) d -> p s d", p=128),
                      writes=["xin%d" % sl])
                for s in range(4):
                    if s % 2 == 0:
                        P.op("act", lambda e, sl=sl, s=s: e.activation(out=xbf[sl][:, s, :], in_=xin[sl][:, s, :], func=AF.Copy),
                             reads=["xin%d" % sl], writes=["xbf%d_%d" % (sl, s)])
                    else:
                        P.op("dve", lambda e, sl=sl, s=s: e.tensor_copy(out=xbf[sl][:, s, :], in_=xin[sl][:, s, :]),
                             reads=["xin%d" % sl], writes=["xbf%d_%d" % (sl, s)])
                    transpose_store(xbf[sl][:, s, :], s % 2, xTs[sl], s * 128, "xbf%d_%d" % (sl, s),
                                    "pst%d" % (s % 2), "xTs%d" % sl, "dve" if s % 2 == 0 else "act")
                P.dma("pool", "xTs%d" % sl, xT_d[:, t0:t0 + 512].rearrange("(c p) n -> p c n", p=128), xTs[sl],
                      reads=["xTs%d" % sl])
            P.barrier()
            A.top = m

        def stageA(l):
            m = A.top
            w_bf = A.bf16(8, INC)
            stg = [A.f32(INC) for _ in range(2)]
            for k in range(8):
                sl = k % 2
                P.dma("sp", "wst%d" % sl, stg[sl], w_in_d[l, k * 128:(k + 1) * 128, :], writes=["wst%d" % sl])
                if k % 2 == 0:
                    P.op("act", lambda e, k=k, sl=sl: e.activation(out=w_bf[:, k, :], in_=stg[sl], func=AF.Copy),
                         reads=["wst%d" % sl], writes=["w_bf"])
                else:
                    P.op("dve", lambda e, k=k, sl=sl: e.tensor_copy(out=w_bf[:, k, :], in_=stg[sl]),
                         reads=["wst%d" % sl], writes=["w_bf"])
            xTt = [A.bf16(8, 512) for _ in range(2)]
            xr_s = [A.f32(4, 512) for _ in range(2)]
            gy_s = [A.f32(4, 512) for _ in range(2)]
            q_s = [A.bf16(4, 512) for _ in range(2)]
            k_s = [A.bf16(512) for _ in range(2)]
            v_s = [A.bf16(4, 128) for _ in range(2)]
            jglob = 0
            for gi, (t0, _, _) in enumerate(tiles512):
                sl = gi % 2
                P.dma("sp", "xTt%d" % sl, xTt[sl], xT_d[:, t0:t0 + 512].rearrange("(c p) n -> p c n", p=128),
                      writes=["xTt%d" % sl])
                for j in range(13):
                    bank = jglob % 6
                    jglob += 1
                    col = j * 128

                    def mm(e, bank=bank, col=col, sl=sl):
                        ins = None
                        for k in range(8):
                            ins = e.matmul(ps[:, bank, :], lhsT=w_bf[:, k, col:col + 128], rhs=xTt[sl][:, k, :],
                                           start=(k == 0), stop=(k == 7))
                        return ins
                    P.op("pe", mm, reads=["w_bf", "xTt%d" % sl], writes=["psA%d" % bank])
                    src = ps[:, bank, :]
                    if j < 4:
                        P.op("act", lambda e, src=src, j=j, sl=sl: e.activation(out=xr_s[sl][:, j, :], in_=src, func=AF.Copy),
                             reads=["psA%d" % bank], writes=["xr_s%d" % sl])
                    elif j < 8:
                        P.op("act", lambda e, src=src, j=j, sl=sl: e.activation(out=gy_s[sl][:, j - 4, :], in_=src, func=AF.Gelu),
                             reads=["psA%d" % bank], writes=["gy_s%d" % sl])
                    elif j < 12:
                        P.op("dve", lambda e, src=src, j=j, sl=sl: e.tensor_scalar(out=q_s[sl][:, j - 8, :], in0=src, scalar1=0.125,
                                                                                  scalar2=None, op0=ALU.mult),
                             reads=["psA%d" % bank], writes=["q_s%d" % sl])
                    else:
                        P.op("dve", lambda e, src=src, sl=sl: e.tensor_copy(out=k_s[sl], in_=src),
                             reads=["psA%d" % bank], writes=["k_s%d" % sl])
                vb = 6 + gi % 2

                def mmv(e, vb=vb, sl=sl):
                    ins = None
                    for s in range(4):
                        for k in range(8):
                            ins = e.matmul(ps[:, vb, s * 128:(s + 1) * 128], lhsT=xTt[sl][:, k, s * 128:(s + 1) * 128],
                                           rhs=w_bf[:, k, 1664:1792], start=(k == 0), stop=(k == 7))
                    return ins
                P.op("pe", mmv, reads=["w_bf", "xTt%d" % sl], writes=["psA%d" % vb])
                P.op("dve", lambda e, vb=vb, sl=sl: e.tensor_copy(out=v_s[sl], in_=ps[:, vb, :].rearrange("p (s c) -> p s c", s=4)),
                     reads=["psA%d" % vb], writes=["v_s%d" % sl])
                fm = lambda d_ap: d_ap[:, t0:t0 + 512].rearrange("(c p) n -> p c n", p=128)
                P.dma("pool", "sxr%d" % sl, fm(xrT_d), xr_s[sl], reads=["xr_s%d" % sl])
                P.dma("pool", "sgy%d" % sl, fm(gyT_d), gy_s[sl], reads=["gy_s%d" % sl])
                P.dma("pool", "sq%d" % sl, fm(qT_d), q_s[sl], reads=["q_s%d" % sl])
                P.dma("pool", "sk%d" % sl, kT_d[:, t0:t0 + 512], k_s[sl], reads=["k_s%d" % sl])
                P.dma("pool", "sv%d" % sl, v_d[t0:t0 + 512, :].rearrange("(s p) c -> p s c", p=128), v_s[sl],
                      reads=["v_s%d" % sl])
            P.barrier()
            A.top = m

        def chanvec(dst, src1d, sem, key):
            P.dma("sp", sem, dst, src1d.rearrange("(c p) -> p c", p=128), writes=[key], allow_slow_non_contiguous=True)

        def rg_setup(l, dr):
            R = {}
            wg32 = A.f32(8, 128)
            R["wg"] = A.bf16(8, 128)
            R["cw"] = A.f32(4, 4)
            R["cb"] = A.f32(4)
            R["ba"] = A.f32(4)
            R["bx"] = A.f32(4)
            lam = A.f32(4)
            R["cneg"] = A.f32(4)
            R["cneg2"] = A.f32(4)
            R["state"] = A.f32(4)
            t1 = A.f32(4)
            t2 = A.f32(4)
            P.op("pool", lambda e: e.memset(wg32, 0.0), writes=["wg32"])
            for gate, wd_ in enumerate((rg_wa_d, rg_wx_d)):
                srcv = wd_[l, dr].rearrange("(cc two) c o -> two c cc o", two=2)
                P.dma("sp", "rgw%d" % gate, wg32[0:64, gate * 4:(gate + 1) * 4, 0:64], srcv[0],
                      reads=["wg32"], writes=["wg32"])
                P.dma("sp", "rgw%d" % (2 + gate), wg32[64:128, gate * 4:(gate + 1) * 4, 64:128], srcv[1],
                      reads=["wg32"], writes=["wg32"])
            P.op("dve", lambda e: e.tensor_copy(out=R["wg"], in_=wg32), reads=["wg32"], writes=["wg"])
            for k in range(4):
                chanvec(R["cw"][:, k, :], rg_conv_w_d[l, k], "rgc%d" % k, "cw")
            chanvec(R["cb"], rg_conv_b_d[l], "rgc4", "cb")
            chanvec(R["ba"], rg_ba_d[l, dr], "rgc5", "ba")
            chanvec(R["bx"], rg_bx_d[l, dr], "rgc6", "bx")
            chanvec(lam, rg_lambda_d[l, dr], "rgc7", "lam")
            P.op("act", lambda e: e.activation(out=t1, in_=lam, func=AF.Exp, scale=-1.0), reads=["lam"], writes=["t1"])
            P.op("dve", lambda e: e.tensor_scalar(out=t2, in0=t1, scalar1=1.0, scalar2=None, op0=ALU.add), reads=["t1"], writes=["t2"])
            P.op("act", lambda e: e.activation(out=lam, in_=t2, func=AF.Ln), reads=["t2"], writes=["lam"])
            P.op("dve", lambda e: e.tensor_scalar(out=t2, in0=t2, scalar1=-1.0, scalar2=1e-30, op0=ALU.add, op1=ALU.add),
                 reads=["t2"], writes=["t2"])
            P.op("dve", lambda e: e.tensor_scalar(out=lam, in0=lam, scalar1=1e-30, scalar2=None, op0=ALU.add), reads=["lam"], writes=["lam"])
            P.op("dve", lambda e: e.reciprocal(out=t2, in_=t2), reads=["t2"], writes=["t2"])
            P.op("dve", lambda e: e.tensor_tensor(out=lam, in0=lam, in1=t2, op=ALU.mult), reads=["lam", "t2"], writes=["lam"])
            P.op("dve", lambda e: e.tensor_tensor(out=lam, in0=lam, in1=t1, op=ALU.mult), reads=["lam", "t1"], writes=["lam"])
            P.op("dve", lambda e: e.tensor_scalar(out=R["cneg"], in0=lam, scalar1=-4.0, scalar2=None, op0=ALU.mult), reads=["lam"], writes=["cneg"])
            P.op("dve", lambda e: e.tensor_scalar(out=R["cneg2"], in0=lam, scalar1=-8.0, scalar2=None, op0=ALU.mult), reads=["lam"], writes=["cneg2"])
            P.op("dve", lambda e: e.tensor_scalar(out=R["ba"], in0=R["ba"], scalar1=0.5, scalar2=None, op0=ALU.mult), reads=["ba"], writes=["ba"])
            P.op("dve", lambda e: e.tensor_scalar(out=R["bx"], in0=R["bx"], scalar1=0.5, scalar2=None, op0=ALU.mult), reads=["bx"], writes=["bx"])
            R["bufs"] = []
            for _ in range(4):
                B = {}
                B["xrw"] = A.f32(515)
                B["u"] = A.f32(512)
                B["ubf"] = A.bf16(512)
                B["r"] = A.f32(512)
                B["i"] = A.f32(512)
                B["a"] = A.f32(512)
                B["s"] = A.f32(512)
                R["bufs"].append(B)
            return R

        def run_interleaved(gens):
            gens = [g for g in gens if g is not None]
            while gens:
                for g in list(gens):
                    try:
                        next(g)
                    except StopIteration:
                        gens.remove(g)

        def rg_chunk(R, dr, cc, t0, s0, s1, hout, hkey, first, bka, bkx):
            B = R["bufs"][cc]
            sl = cc
            kx = "rgx%d" % sl
            lo = max(t0 - 2, s0)
            hi = min(t0 + 513, s1)
            if lo > t0 - 2:
                P.op("pool", lambda e, B=B: e.memset(B["xrw"][:, 0:2], 0.0), writes=[kx])
            if hi < t0 + 513:
                P.op("pool", lambda e, B=B: e.memset(B["xrw"][:, 514:515], 0.0), writes=[kx])
            P.dma("sp", kx, B["xrw"][:, lo - (t0 - 2):hi - (t0 - 2)], xrT_d[cc * 128:(cc + 1) * 128, lo:hi],
                  reads=[kx], writes=[kx])
            yield
            cw, cb = R["cw"], R["cb"]
            ku = "rgu%d" % sl
            P.op("dve", lambda e, B=B: e.tensor_scalar(out=B["u"], in0=B["xrw"][:, 0:512], scalar1=cw[:, 0, cc:cc + 1],
                                                      scalar2=cb[:, cc:cc + 1], op0=ALU.mult, op1=ALU.add),
                 reads=[kx, "cw", "cb"], writes=[ku])
            yield
            for k in range(1, 4):
                P.op("dve", lambda e, B=B, k=k: e.scalar_tensor_tensor(out=B["u"], in0=B["xrw"][:, k:k + 512],
                                                                      scalar=cw[:, k, cc:cc + 1], in1=B["u"],
                                                                      op0=ALU.mult, op1=ALU.add),
                     reads=[kx, ku, "cw"], writes=[ku])
                yield
            P.op("act", lambda e, B=B: e.activation(out=B["ubf"], in_=B["u"], func=AF.Copy), reads=[ku], writes=["rgub%d" % sl])
            yield
            ka, kg = "psg%d" % bka, "psg%d" % bkx
            P.op("pe", lambda e, B=B: e.matmul(ps[:, bka, :], lhsT=R["wg"][:, cc, :], rhs=B["ubf"], start=True, stop=True),
                 reads=["rgub%d" % sl, "wg"], writes=[ka])
            P.op("act", lambda e, B=B: e.activation(out=B["r"], in_=ps[:, bka, :], func=AF.Tanh, bias=R["ba"][:, cc:cc + 1], scale=0.5),
                 reads=[ka, "ba"], writes=["rgr%d" % sl])
            P.op("pe", lambda e, B=B: e.matmul(ps[:, bkx, :], lhsT=R["wg"][:, 4 + cc, :], rhs=B["ubf"], start=True, stop=True),
                 reads=["rgub%d" % sl, "wg"], writes=[kg])
            P.op("act", lambda e, B=B: e.activation(out=B["i"], in_=ps[:, bkx, :], func=AF.Tanh, bias=R["bx"][:, cc:cc + 1], scale=0.5),
                 reads=[kg, "bx"], writes=["rgi%d" % sl])
            yield
            P.op("act", lambda e, B=B: e.activation(out=B["a"], in_=B["r"], func=AF.Exp, scale=R["cneg"][:, cc:cc + 1],
                                                    bias=R["cneg"][:, cc:cc + 1]),
                 reads=["rgr%d" % sl, "cneg"], writes=["rga%d" % sl])
            yield
            P.op("act", lambda e, B=B: e.activation(out=B["s"], in_=B["r"], func=AF.Exp, scale=R["cneg2"][:, cc:cc + 1],
                                                    bias=R["cneg2"][:, cc:cc + 1]),
                 reads=["rgr%d" % sl, "cneg2"], writes=["rgs%d" % sl])
            yield
            P.op("dve", lambda e, B=B: e.tensor_scalar(out=B["s"], in0=B["s"], scalar1=1.0, scalar2=None, op0=ALU.min),
                 reads=["rgs%d" % sl], writes=["rgs%d" % sl])
            yield
            P.op("act", lambda e, B=B: e.activation(out=B["s"], in_=B["s"], func=AF.Sqrt, scale=-0.25, bias=quart_t),
                 reads=["rgs%d" % sl, "eps"], writes=["rgs%d" % sl])
            yield
            P.op("dve", lambda e, B=B: e.scalar_tensor_tensor(out=B["s"], in0=B["i"], scalar=1.0, in1=B["s"], op0=ALU.add, op1=ALU.mult),
                 reads=["rgs%d" % sl, "rgi%d" % sl], writes=["rgs%d" % sl])
            yield
            P.op("dve", lambda e, B=B: e.tensor_tensor(out=B["s"], in0=B["s"], in1=B["u"], op=ALU.mult),
                 reads=["rgs%d" % sl, ku], writes=["rgs%d" % sl])
            yield
            stc = R["state"][:, cc:cc + 1]
            if first:
                P.op("pool", lambda e: e.memset(stc, 0.0), writes=["rgst%d" % cc])
            if dr == 0:
                P.op("dve", lambda e, B=B: e.tensor_tensor_scan(out=hout, data0=B["a"], data1=B["s"], initial=stc,
                                                               op0=ALU.mult, op1=ALU.add),
                     reads=["rga%d" % sl, "rgs%d" % sl, "rgst%d" % cc], writes=[hkey])
                yield
                P.op("pool", lambda e: e.tensor_copy(out=stc, in_=hout[:, 511:512]), reads=[hkey], writes=["rgst%d" % cc])
            else:
                P.op("dve", lambda e, B=B: e.tensor_tensor_scan(out=hout[:, ::-1], data0=B["a"][:, ::-1], data1=B["s"][:, ::-1],
                                                               initial=stc, op0=ALU.mult, op1=ALU.add),
                     reads=["rga%d" % sl, "rgs%d" % sl, "rgst%d" % cc], writes=[hkey])
                yield
                P.op("pool", lambda e: e.tensor_copy(out=stc, in_=hout[:, 0:1]), reads=[hkey], writes=["rgst%d" % cc])
            yield

        def stageB(l):
            m = A.top
            R = rg_setup(l, 1)
            hst = [A.f32(4, 512) for _ in range(2)]
            wi = 0
            for (s0, n) in seqs:
                s1 = s0 + n
                for t0 in range(s1 - 512, s0 - 1, -512):
                    sl = wi % 2
                    wi += 1
                    run_interleaved([rg_chunk(R, 1, cc, t0, s0, s1, hst[sl][:, cc, :], "hst%d_%d" % (sl, cc),
                                              first=(t0 == s1 - 512), bka=2 * cc, bkx=2 * cc + 1) for cc in range(4)])
                    P.dma("pool", "shb%d" % sl, hbT_d[:, t0:t0 + 512].rearrange("(c p) n -> p c n", p=128), hst[sl],
                          reads=["hst%d_%d" % (sl, cc) for cc in range(4)])
            P.barrier()
            A.top = m

        def layer_norm_g(y, ykey, g_bc, b_bc, gkey, st6, mv, rstd, tag):
            for hh in range(2):
                P.op("dve", lambda e, hh=hh: e.bn_stats(out=st6[:, hh * 6:(hh + 1) * 6], in_=y[:, hh * 512:(hh + 1) * 512]),
                     reads=[ykey], writes=[tag + "st%d" % hh])
            yield
            P.op("dve", lambda e: e.bn_aggr(out=mv, in_=st6), reads=[tag + "st0", tag + "st1"], writes=[tag + "mv"])
            yield
            P.op("act", lambda e: e.activation(out=rstd, in_=mv[:, 1:2], func=AF.Sqrt, bias=eps_ln, scale=1.0),
                 reads=[tag + "mv", "eps"], writes=[tag + "rs"])
            yield
            P.op("dve", lambda e: e.reciprocal(out=rstd, in_=rstd), reads=[tag + "rs"], writes=[tag + "rs"])
            yield
            P.op("dve", lambda e: e.tensor_scalar(out=y, in0=y, scalar1=mv[:, 0:1], scalar2=rstd, op0=ALU.subtract, op1=ALU.mult),
                 reads=[ykey, tag + "mv", tag + "rs"], writes=[ykey])
            yield
            P.op("pool", lambda e: e.tensor_tensor(out=y, in0=y, in1=g_bc, op=ALU.mult), reads=[ykey, gkey], writes=[ykey])
            yield
            P.op("pool", lambda e: e.tensor_tensor(out=y, in0=y, in1=b_bc, op=ALU.add), reads=[ykey, gkey], writes=[ykey])
            yield

        def layer_norm(y, ykey, g_bc, b_bc, gkey, st6, mv, rstd, tag):
            for hh in range(2):
                P.op("dve", lambda e, hh=hh: e.bn_stats(out=st6[:, hh * 6:(hh + 1) * 6], in_=y[:, hh * 512:(hh + 1) * 512]),
                     reads=[ykey], writes=[tag + "st%d" % hh])
            P.op("dve", lambda e: e.bn_aggr(out=mv, in_=st6), reads=[tag + "st0", tag + "st1"], writes=[tag + "mv"])
            P.op("act", lambda e: e.activation(out=rstd, in_=mv[:, 1:2], func=AF.Sqrt, bias=eps_ln, scale=1.0),
                 reads=[tag + "mv", "eps"], writes=[tag + "rs"])
            P.op("dve", lambda e: e.reciprocal(out=rstd, in_=rstd), reads=[tag + "rs"], writes=[tag + "rs"])
            P.op("dve", lambda e: e.tensor_scalar(out=y, in0=y, scalar1=mv[:, 0:1], scalar2=rstd, op0=ALU.subtract, op1=ALU.mult),
                 reads=[ykey, tag + "mv", tag + "rs"], writes=[ykey])
            P.op("pool", lambda e: e.tensor_tensor(out=y, in0=y, in1=g_bc, op=ALU.mult), reads=[ykey, gkey], writes=[ykey])
            P.op("pool", lambda e: e.tensor_tensor(out=y, in0=y, in1=b_bc, op=ALU.add), reads=[ykey, gkey], writes=[ykey])

        def stageC(l):
            m = A.top
            wo = A.bf16(8, D)
            gcol = A.f32(8)
            g_bc = A.f32(D)
            b_bc = A.f32(D)
            m_tmp = A.top
            chanvec(gcol[:, 0:4], norm_rg_g_d[l], "cg0", "gcol")
            chanvec(gcol[:, 4:8], norm_attn_g_d[l], "cg1", "gcol")
            stg = [A.f32(D) for _ in range(2)]
            for k in range(8):
                sl = k % 2
                P.dma("sp", "wst%d" % sl, stg[sl], w_out_d[l, k * 128:(k + 1) * 128, :], writes=["wst%d" % sl])
                P.op("dve", lambda e, k=k, sl=sl: e.tensor_scalar(out=wo[:, k, :], in0=stg[sl], scalar1=gcol[:, k:k + 1], scalar2=None,
                                                              op0=ALU.mult),
                     reads=["wst%d" % sl, "gcol"], writes=["wo"])
            P.dma("sp", "cg2", g_bc, ln1_g_d[l].partition_broadcast(128), writes=["lng"])
            P.dma("sp", "cg3", b_bc, ln1_b_d[l].partition_broadcast(128), writes=["lng"])
            P.barrier()
            A.top = m_tmp
            R = rg_setup(l, 0)
            hb_s = A.f32(4, 512)
            gy_s = A.f32(4, 512)
            hf = [A.f32(512) for _ in range(4)]
            sq_bf = A.bf16(4, 512)
            rg_bf = [A.bf16(4, 512) for _ in range(2)]
            at_bf = [A.bf16(4, 512) for _ in range(2)]
            qt = [A.bf16(4, 512) for _ in range(2)]
            kd = [A.bf16(2, 768) for _ in range(2)]
            va = [A.bf16(6, 2, 96) for _ in range(2)]
            PT = [A.bf16(512) for _ in range(6)]
            den = A.f32(8)
            o_bf = A.bf16(512)
            junk = A.bf16(512)
            ssa = A.f32(4)
            rs_rg = [A.f32(4) for _ in range(2)]
            rs_at = [A.f32(4) for _ in range(2)]
            xt = [A.f32(D) for _ in range(2)]
            yb = [A.f32(D) for _ in range(2)]
            x1bf = [A.bf16(D) for _ in range(2)]
            x1Ts = A.bf16(8, 512)
            st6 = [A.f32(12) for _ in range(2)]
            mv = [A.f32(2) for _ in range(2)]
            rstd = [A.f32(1) for _ in range(2)]
            for sl in range(2):
                P.op("pool", lambda e, sl=sl: e.memset(va[sl][:, :, :, 64:65], 1.0), writes=["va%d" % sl])
            P.op("pool", lambda e: e.memset(qt[0][64:128, :, :], 0.0), writes=["qt"])
            P.op("pool", lambda e: e.memset(qt[1][0:64, :, :], 0.0), writes=["qt"])
            po = ps[:, 3:5, 0:260].rearrange("p b (h e) -> p b h e", h=4)
            ps_ss = ps[:, 3, 384:388]
            pT = psbf(5)
            ctr = {"pt": 0, "sub": 0}

            def g_rg(cc, t0, s0, s1, par):
                hk = "hf%d" % cc
                yield from rg_chunk(R, 0, cc, t0, s0, s1, hf[cc], hk, first=(t0 == s0), bka=0, bkx=0)
                P.op("pool", lambda e: e.tensor_tensor(out=hf[cc], in0=hf[cc], in1=hb_s[:, cc, :], op=ALU.add),
                     reads=[hk, "hbs"], writes=[hk])
                yield
                P.op("dve", lambda e: e.tensor_tensor(out=hf[cc], in0=hf[cc], in1=gy_s[:, cc, :], op=ALU.mult),
                     reads=[hk, "gys"], writes=[hk])
                yield
                P.op("act", lambda e: e.activation(out=sq_bf[:, cc, :], in_=hf[cc], func=AF.Square),
                     reads=[hk], writes=["sq%d" % cc])
                yield
                P.op("pool", lambda e: e.tensor_copy(out=rg_bf[par][:, cc, :], in_=hf[cc]),
                     reads=[hk], writes=["rgbf%d_%d" % (par, cc)])
                yield

            def g_attn(t0, s0, s1, par):
                nb = (s1 - s0) // 128
                sl = par
                fm = lambda d_ap: d_ap[:, t0:t0 + 512].rearrange("(c p) n -> p c n", p=128)
                P.dma("sp", "qt", qt[0][0:64, :, :], fm(qT_d)[0:64], reads=["qt"], writes=["qt"])
                P.dma("sp", "qt", qt[1][64:128, :, :], fm(qT_d)[64:128], reads=["qt"], writes=["qt"])
                klo = max(t0 - 128, s0)
                khi = min(t0 + 640, s1)
                o0 = klo - (t0 - 128)
                for kvh in range(2):
                    for half in range(2):
                        P.dma("sp", "kd%d" % sl, kd[sl][half * 64:(half + 1) * 64, kvh, o0:o0 + khi - klo],
                              kT_d[kvh * 64:(kvh + 1) * 64, klo:khi], writes=["kd%d" % sl])
                for kvh in range(2):
                    P.dma("sp", "va%d" % sl, va[sl][:, o0 // 128:(o0 + khi - klo) // 128, kvh, 0:64],
                          v_d[klo:khi, kvh * 64:(kvh + 1) * 64].rearrange("(s p) d -> p s d", p=128),
                          reads=["va%d" % sl], writes=["va%d" % sl])
                yield
                for j in range(4):
                    jb = (t0 - s0) // 128 + j
                    rs_valid = [r for r in range(3) if 0 <= jb + r - 1 < nb]
                    for hg in range(2):
                        pts = {}
                        for r in rs_valid:
                            kc = (j + r) * 128
                            pt = PT[ctr["pt"] % 6]
                            ptk = "PT%d" % (ctr["pt"] % 6)
                            ctr["pt"] += 1
                            pts[r] = (pt, ptk)

                            sb = 1 + ctr["pt"] % 2
                            skey = "psS%d" % sb

                            def mqk(e, r=r, hg=hg, kc=kc, j=j, sl=sl, sb=sb):
                                ins = e.matmul(ps[:, sb, :], lhsT=ident, rhs=biasT[:, r, hg, :], start=True, stop=False)
                                for h4 in range(4):
                                    head = hg * 4 + h4
                                    cq, half = head // 2, head % 2
                                    ins = e.matmul(ps[:, sb, h4 * 128:(h4 + 1) * 128], lhsT=kd[sl][:, hg, kc:kc + 128],
                                                   rhs=qt[half][:, cq, j * 128:(j + 1) * 128], start=False, stop=(h4 == 3))
                                return ins
                            P.op("pe", mqk, reads=["ident", "biasT", "kd%d" % sl, "qt"], writes=[skey])
                            yield
                            P.op("act", lambda e, pt=pt, sb=sb: e.activation(out=pt, in_=ps[:, sb, :], func=AF.Exp),
                                 reads=[skey], writes=[ptk])
                            yield

                        def mpv(e, hg=hg, j=j, sl=sl, pts=pts, rs_valid=rs_valid):
                            ins = None
                            for h4 in range(4):
                                for r in rs_valid:
                                    ins = e.matmul(po[:, hg, h4, :], lhsT=pts[r][0][:, h4 * 128:(h4 + 1) * 128],
                                                   rhs=va[sl][:, j + r, hg, 0:65], start=(r == rs_valid[0]), stop=(r == rs_valid[-1]))
                            return ins
                        P.op("pe", mpv, reads=[pts[r][1] for r in rs_valid] + ["va%d" % sl], writes=["po%d" % hg])
                        yield
                    P.op("dve", lambda e: e.tensor_tensor(out=den.rearrange("p (b h) -> p b h", b=2), in0=po[:, :, :, 64],
                                                          in1=esink[:, l, :].rearrange("p (b h) -> p b h", b=2), op=ALU.add),
                         reads=["po0", "po1", "esink"], writes=["den"])
                    yield
                    P.op("dve", lambda e: e.reciprocal(out=den, in_=den), reads=["den"], writes=["den"])
                    yield
                    P.op("dve", lambda e: e.tensor_tensor(out=o_bf.rearrange("p (b h d) -> p b h d", b=2, h=4), in0=po[:, :, :, 0:64],
                                                          in1=den.rearrange("p (b h) -> p b h", b=2).unsqueeze(3).to_broadcast([128, 2, 4, 64]),
                                                          op=ALU.mult),
                         reads=["po0", "po1", "den"], writes=["o_bf"])
                    yield
                    def mtr(e):
                        ins = None
                        for c in range(4):
                            ins = e.transpose(out=pT[:, c, :], in_=o_bf[:, c * 128:(c + 1) * 128], identity=ident)
                        return ins
                    P.op("pe", mtr, reads=["o_bf", "ident"], writes=["psT"])
                    P.op("dve", lambda e, j=j: e.tensor_copy(out=at_bf[par][:, :, j * 128:(j + 1) * 128], in_=pT[:, 0:4, :]),
                         reads=["psT"], writes=["atbf%d_%d" % (par, j)])
                    yield
                    P.op("pool", lambda e, j=j: e.memset(ssa[:, j:j + 1], 0.0), writes=["ssa%d" % j])
                    P.op("act", lambda e, j=j: e.activation(out=junk, in_=o_bf, func=AF.Square, accum_out=ssa[:, j:j + 1]),
                         reads=["o_bf"], writes=["junk", "ssa%d" % j])
                    yield
                    rsa = rs_at[par][:, j:j + 1]
                    rk = "rs_at%d_%d" % (par, j)
                    P.op("act", lambda e, j=j, rsa=rsa: e.activation(out=rsa, in_=ssa[:, j:j + 1], func=AF.Sqrt, scale=1.0 / 512, bias=eps_rms),
                         reads=["ssa%d" % j, "eps"], writes=[rk])
                    yield
                    P.op("dve", lambda e, rsa=rsa: e.reciprocal(out=rsa, in_=rsa), reads=[rk], writes=[rk])
                    yield


            def g_out(t0, par, subs, q):
                for s in subs:
                    tt = t0 + s * 128
                    src_x = x_d if l == 0 else xres_d
                    xk = "xt%d" % q
                    yk = "yb%d" % q
                    P.dma("sp", xk, xt[q], src_x[tt:tt + 128, :], writes=[xk])
                    yield
                    P.op("act", lambda e: e.activation(out=xt[q], in_=xt[q], func=AF.Copy, scale=ALPHA),
                         reads=[xk], writes=[xk])
                    yield

                    def mwo(e, s=s, src=rg_bf[par], kb=0):
                        ins = None
                        for hh in range(2):
                            for c in range(4):
                                ins = e.matmul(ps[:, 6 + hh, :], lhsT=src[:, c, s * 128:(s + 1) * 128],
                                               rhs=wo[:, kb + c, hh * 512:(hh + 1) * 512], start=(c == 0), stop=(c == 3))
                        return ins
                    pw = ps[:, 6:8, :]
                    yv = yb[q].rearrange("p (b n) -> p b n", b=2)
                    xv = xt[q].rearrange("p (b n) -> p b n", b=2)
                    P.op("pe", mwo, reads=["rgbf%d_%d" % (par, c) for c in range(4)] + ["wo"], writes=["psW"])
                    P.op("dve", lambda e, s=s, yv=yv, xv=xv: e.scalar_tensor_tensor(out=yv, in0=pw, scalar=rs_rg[par][:, s:s + 1], in1=xv,
                                                                                  op0=ALU.mult, op1=ALU.add),
                         reads=["psW", "rs_rg%d" % par, xk], writes=[yk])
                    yield
                    P.op("pe", lambda e, s=s, mwo=mwo: mwo(e, s=s, src=at_bf[par], kb=4), reads=["atbf%d_%d" % (par, s), "wo"], writes=["psW"])
                    P.op("dve", lambda e, s=s, yv=yv: e.scalar_tensor_tensor(out=yv, in0=pw, scalar=rs_at[par][:, s:s + 1], in1=yv,
                                                                           op0=ALU.mult, op1=ALU.add),
                         reads=["psW", "rs_at%d_%d" % (par, s), yk], writes=[yk])
                    yield
                    yield from layer_norm_g(yb[q], yk, g_bc, b_bc, "lng", st6[q], mv[q], rstd[q], "ln%d" % q)
                    P.dma("pool", "sx1_%d" % q, x1_d[tt:tt + 128, :], yb[q], reads=[yk])
                    P.op("act", lambda e: e.activation(out=x1bf[q], in_=yb[q], func=AF.Copy), reads=[yk], writes=["x1bf%d" % q])
                    yield
                    transpose_store(x1bf[q], 5, x1Ts, s * 128, "x1bf%d" % q, "psT", "x1Ts_%d" % s, "act")
                    yield

            wins = []
            for (s0, n) in seqs:
                for t0 in range(s0, s0 + n, 512):
                    wins.append((t0, s0, s0 + n))
            prev = None
            for wi, (t0, s0, s1) in enumerate(wins):
                par = wi % 2
                fm = lambda d_ap: d_ap[:, t0:t0 + 512].rearrange("(c p) n -> p c n", p=128)
                P.dma("sp", "hbs", hb_s, fm(hbT_d), writes=["hbs"])
                P.dma("sp", "gys", gy_s, fm(gyT_d), writes=["gys"])
                gens = [g_rg(cc, t0, s0, s1, par) for cc in range(4)]
                gens.append(g_attn(t0, s0, s1, par))
                if prev is not None:
                    gens.append(g_out(prev[0], prev[1], (0, 2), 0))
                    gens.append(g_out(prev[0], prev[1], (1, 3), 1))
                run_interleaved(gens)
                if prev is not None:
                    P.dma("pool", "sx1T0", x1T_d[:, prev[0]:prev[0] + 512].rearrange("(c p) n -> p c n", p=128), x1Ts,
                          reads=["x1Ts_%d" % s for s in range(4)])

                def mss(e):
                    ins = None
                    for s in range(4):
                        for cc in range(4):
                            ins = e.matmul(ps_ss[:, s:s + 1], lhsT=sq_bf[:, cc, s * 128:(s + 1) * 128], rhs=ones_bf[:, 0:1],
                                           start=(cc == 0), stop=(cc == 3))
                    return ins
                P.op("pe", mss, reads=["sq%d" % c for c in range(4)] + ["ones"], writes=["psss"])
                P.op("act", lambda e, par=par: e.activation(out=rs_rg[par], in_=ps_ss, func=AF.Sqrt, scale=1.0 / 512, bias=eps_rms),
                     reads=["psss", "eps"], writes=["rs_rg%d" % par])
                P.op("dve", lambda e, par=par: e.reciprocal(out=rs_rg[par], in_=rs_rg[par]), reads=["rs_rg%d" % par], writes=["rs_rg%d" % par])
                prev = (t0, par)
            run_interleaved([g_out(prev[0], prev[1], (0, 2), 0), g_out(prev[0], prev[1], (1, 3), 1)])
            P.dma("pool", "sx1T0", x1T_d[:, prev[0]:prev[0] + 512].rearrange("(c p) n -> p c n", p=128), x1Ts,
                  reads=["x1Ts_%d" % s for s in range(4)])
            P.barrier()
            A.top = m

        def stageD(l, last):
            m = A.top
            wgu = A.bf16(8, 2 * DFF)
            wd = A.bf16(NFF, D)
            fcw = A.f32(3, NFF)
            fcb = A.f32(NFF)
            g_bc = A.f32(D)
            b_bc = A.f32(D)
            m2 = A.top
            stg = [A.f32(1408) for _ in range(3)]
            ci = 0
            for k in range(8):
                for pc in range(4):
                    sl = ci % 3
                    P.dma("sp", "wst%d" % sl, stg[sl], ffn_w_in_d[l, k * 128:(k + 1) * 128, pc * 1408:(pc + 1) * 1408],
                          writes=["wst%d" % sl])
                    dst = wgu[:, k, pc * 1408:(pc + 1) * 1408]
                    if ci % 3 == 0:
                        P.op("act", lambda e, dst=dst, sl=sl: e.activation(out=dst, in_=stg[sl], func=AF.Copy), reads=["wst%d" % sl], writes=["wgu"])
                    elif ci % 3 == 1:
                        P.op("dve", lambda e, dst=dst, sl=sl: e.tensor_copy(out=dst, in_=stg[sl]), reads=["wst%d" % sl], writes=["wgu"])
                    else:
                        P.op("pool", lambda e, dst=dst, sl=sl: e.tensor_copy(out=dst, in_=stg[sl]), reads=["wst%d" % sl], writes=["wgu"])
                    ci += 1
            for c in range(NFF):
                sl = ci % 3
                P.dma("sp", "wst%d" % sl, stg[sl][:, 0:D], ffn_w_out_d[l, c * 128:(c + 1) * 128, :], writes=["wst%d" % sl])
                dst = wd[:, c, :]
                if ci % 3 == 0:
                    P.op("act", lambda e, dst=dst, sl=sl: e.activation(out=dst, in_=stg[sl][:, 0:D], func=AF.Copy), reads=["wst%d" % sl], writes=["wd"])
                elif ci % 3 == 1:
                    P.op("dve", lambda e, dst=dst, sl=sl: e.tensor_copy(out=dst, in_=stg[sl][:, 0:D]), reads=["wst%d" % sl], writes=["wd"])
                else:
                    P.op("pool", lambda e, dst=dst, sl=sl: e.tensor_copy(out=dst, in_=stg[sl][:, 0:D]), reads=["wst%d" % sl], writes=["wd"])
                ci += 1
            for k in range(3):
                chanvec(fcw[:, k, :], ffn_conv_w_d[l, k], "fc%d" % k, "fcw")
            chanvec(fcb, ffn_conv_b_d[l], "fc3", "fcb")
            P.dma("sp", "cg2", g_bc, ln2_g_d[l].partition_broadcast(128), writes=["lng"])
            P.dma("sp", "cg3", b_bc, ln2_b_d[l].partition_broadcast(128), writes=["lng"])
            P.barrier()
            A.top = m2
            TT = 256
            x1t = [A.bf16(8, TT + 2) for _ in range(2)]
            hT = A.bf16(NFF, TT)
            acc = [A.f32(TT) for _ in range(2)]
            gl = [A.f32(TT) for _ in range(2)]
            xt = [A.f32(D) for _ in range(1)]
            yb = [A.f32(D) for _ in range(2)]
            xbf = A.bf16(D)
            xTs = [A.bf16(8, TT) for _ in range(1)]
            st6 = A.f32(12)
            mv = A.f32(2)
            rstd = A.f32(1)
            ti = 0
            sub_i = 0
            cgl = 0
            for (s0, n) in seqs:
                s1 = s0 + n
                for t0 in range(s0, s1, TT):
                    sl = ti % 2
                    ti += 1
                    kx = "x1t%d" % sl
                    lo = max(t0 - 1, s0)
                    hi = min(t0 + TT + 1, s1)
                    if lo > t0 - 1:
                        P.op("pool", lambda e, sl=sl: e.memset(x1t[sl][:, :, 0:1], 0.0), writes=[kx])
                    if hi < t0 + TT + 1:
                        P.op("pool", lambda e, sl=sl: e.memset(x1t[sl][:, :, TT + 1:TT + 2], 0.0), writes=[kx])
                    P.dma("sp", kx, x1t[sl][:, :, lo - (t0 - 1):hi - (t0 - 1)],
                          x1T_d[:, lo:hi].rearrange("(c p) n -> p c n", p=128), reads=[kx], writes=[kx])
                    for c in range(NFF):
                        pb = cgl % 2
                        cgl += 1

                        def mg(e, c=c, pb=pb, sl=sl):
                            ins = None
                            for k in range(8):
                                ins = e.matmul(ps[:, pb, 0:TT + 2], lhsT=wgu[:, k, c * 128:(c + 1) * 128], rhs=x1t[sl][:, k, :],
                                               start=(k == 0), stop=(k == 7))
                            return ins

                        def mu(e, c=c, pb=pb, sl=sl):
                            ins = None
                            for k in range(8):
                                ins = e.matmul(ps[:, 2 + pb, 0:TT], lhsT=wgu[:, k, DFF + c * 128:DFF + (c + 1) * 128],
                                               rhs=x1t[sl][:, k, 1:TT + 1], start=(k == 0), stop=(k == 7))
                            return ins
                        P.op("pe", mg, reads=["wgu", kx], writes=["psg%d" % pb])
                        P.op("pe", mu, reads=["wgu", kx], writes=["psu%d" % pb])
                        ak = "acc%d" % pb
                        P.op("dve", lambda e, c=c, pb=pb: e.tensor_scalar(out=acc[pb], in0=ps[:, pb, 1:TT + 1], scalar1=fcw[:, 1, c:c + 1],
                                                                        scalar2=fcb[:, c:c + 1], op0=ALU.mult, op1=ALU.add),
                             reads=["psg%d" % pb, "fcw", "fcb"], writes=[ak])
                        P.op("dve", lambda e, c=c, pb=pb: e.scalar_tensor_tensor(out=acc[pb], in0=ps[:, pb, 0:TT], scalar=fcw[:, 0, c:c + 1],
                                                                               in1=acc[pb], op0=ALU.mult, op1=ALU.add),
                             reads=["psg%d" % pb, "fcw", ak], writes=[ak])
                        P.op("dve", lambda e, c=c, pb=pb: e.scalar_tensor_tensor(out=acc[pb], in0=ps[:, pb, 2:TT + 2], scalar=fcw[:, 2, c:c + 1],
                                                                               in1=acc[pb], op0=ALU.mult, op1=ALU.add),
                             reads=["psg%d" % pb, "fcw", ak], writes=[ak])
                        P.op("act", lambda e, pb=pb: e.activation(out=gl[pb], in_=acc[pb], func=AF.Gelu), reads=[ak], writes=["gl%d" % pb])
                        P.op("dve", lambda e, c=c, pb=pb: e.tensor_tensor(out=hT[:, c, :], in0=gl[pb], in1=ps[:, 2 + pb, 0:TT], op=ALU.mult),
                             reads=["gl%d" % pb, "psu%d" % pb], writes=["hT%d" % c])
                    for s in range(TT // 128):
                        xs = sub_i % 2
                        xq = 0
                        sub_i += 1
                        tt = t0 + s * 128

                        def mdn(e, s=s):
                            ins = None
                            for hh in range(2):
                                for c in range(NFF):
                                    ins = e.matmul(ps[:, 4 + 2 * (s % 2) + hh, :], lhsT=hT[:, c, s * 128:(s + 1) * 128],
                                                   rhs=wd[:, c, hh * 512:(hh + 1) * 512], start=(c == 0), stop=(c == NFF - 1))
                            return ins
                        P.op("pe", mdn, reads=["hT%d" % c for c in range(NFF)] + ["wd"], writes=["psd%d" % (s % 2)])
                        P.dma("sp", "xt%d" % xq, xt[xq], x1_d[tt:tt + 128, :], writes=["xt%d" % xq])
                        P.op("act", lambda e, xq=xq: e.activation(out=xt[xq], in_=xt[xq], func=AF.Copy, scale=ALPHA),
                             reads=["xt%d" % xq], writes=["xt%d" % xq])
                        b0 = 4 + 2 * (s % 2)
                        pw = ps[:, b0:b0 + 2, :]
                        yv = yb[xs].rearrange("p (b n) -> p b n", b=2)
                        xv = xt[xq].rearrange("p (b n) -> p b n", b=2)
                        P.op("dve", lambda e, pw=pw, yv=yv, xv=xv: e.tensor_tensor(out=yv, in0=pw, in1=xv, op=ALU.add),
                             reads=["psd%d" % (s % 2), "xt%d" % xq], writes=["yb%d" % xs])
                        layer_norm(yb[xs], "yb%d" % xs, g_bc, b_bc, "lng", st6, mv, rstd, "ln")
                        if last:
                            P.dma("pool", "sy%d" % xs, y_d[tt:tt + 128, :], yb[xs], reads=["yb%d" % xs])
                        else:
                            P.dma("pool", "sy%d" % xs, xres_d[tt:tt + 128, :], yb[xs], reads=["yb%d" % xs])
                            P.op("act", lambda e, xs=xs: e.activation(out=xbf, in_=yb[xs], func=AF.Copy), reads=["yb%d" % xs], writes=["xbf"])
                            transpose_store(xbf, s % 2, xTs[0], s * 128, "xbf", "psg%d" % (s % 2), "xTs0", "act")
                    if not last:
                        P.dma("pool", "sxT0", xT_d[:, t0:t0 + TT].rearrange("(c p) n -> p c n", p=128), xTs[0],
                              reads=["xTs0"])
            P.barrier()
            A.top = m

        stage0()
        for l in range(depth):
            stageA(l)
            stageB(l)
            stageC(l)
            stageD(l, l == depth - 1)
        P.emit()
    return nc


SEQS = [(0, 2048), (2048, 2048), (4096, 16384)]
_WNAMES = ["w_in", "rg_conv_w", "rg_conv_b", "rg_wa", "rg_ba", "rg_wx", "rg_bx", "rg_lambda", "attn_sink", "rel_bias",
           "norm_rg_g", "norm_attn_g", "w_out", "ln1_g", "ln1_b", "ffn_w_in", "ffn_conv_w", "ffn_conv_b", "ffn_w_out",
           "ln2_g", "ln2_b"]


def kernel(**inputs):
    x_prompt = np.asarray(inputs["x_prompt"], dtype=np.float32)
    x_sample = np.asarray(inputs["x_sample"], dtype=np.float32)
    depth = int(np.asarray(inputs["w_in"]).shape[0])
    nc = build_program(SEQS, depth)
    shared = {k: np.ascontiguousarray(np.asarray(inputs[k], dtype=np.float32)) for k in _WNAMES}
    shared["c_ident"] = np.eye(128, dtype=np.float32)
    shared["c_onehot"] = _band_onehot()
    in_maps = []
    for c in range(8):
        xs = np.concatenate([x_prompt[2 * c], x_prompt[2 * c + 1], x_sample[c % 2]], axis=0)
        mm = dict(shared)
        mm["x"] = np.ascontiguousarray(xs)
        in_maps.append(mm)
    res = run_bass_kernel_spmd(nc, in_maps, core_ids=list(range(8)))
    y_prompt = np.empty_like(x_prompt)
    y_sample = np.empty_like(x_sample)
    for c in range(8):
        y = np.asarray(res.results[c]["y"], dtype=np.float32)
        y_prompt[2 * c] = y[0:2048]
        y_prompt[2 * c + 1] = y[2048:4096]
        if c < 2:
            y_sample[c] = y[4096:20480]
    return (y_prompt, y_sample)
```
